# Optimizing a Trainium2 kernel written in Bass

```python
import math
import jax, jax.numpy as jnp
from jax import lax
import numpy as np

D_MODEL = 2048
BATCH = 16
SEQ = 256
DEPTH = 4
DEC_BATCH = 2
DEC_SEQ = 1024
PAST_LEN = 256

GRID_W = 64
CONV_WIDTH = 1024
CONV_HEADS = 16
SSM_WIDTH = 1024
SSM_GROUP = 16
SSM_GROUPS = SSM_WIDTH // SSM_GROUP
SSM_STATE = 64
MIX_WIDTH = CONV_WIDTH + SSM_WIDTH
IN_COLS = 3 * CONV_WIDTH + SSM_WIDTH
D_FF = 5632
EPS = 1e-6
DT_MIN = 1e-3
DT_MAX = 1e-1

kernel_name = 'hybrid_conv_s5_diffusion_step'


def _rms(x, g):
    xf = x.astype(jnp.float32)
    y = xf * lax.rsqrt(jnp.mean(xf * xf, axis=-1, keepdims=True) + EPS)
    return (y * g.astype(jnp.float32)).astype(x.dtype)


def _head_rms(x, g):
    hd = CONV_WIDTH // CONV_HEADS
    xh = x.reshape(x.shape[:-1] + (CONV_HEADS, hd))
    return _rms(xh, g.reshape(CONV_HEADS, hd)).reshape(x.shape)


def _dwconv3(x, w, axis):
    n = x.shape[axis]
    pad = [(0, 0)] * x.ndim
    pad[axis] = (1, 1)
    xp = jnp.pad(x, pad)
    return (lax.slice_in_dim(xp, 0, n, axis=axis) * w[0]
            + lax.slice_in_dim(xp, 1, n + 1, axis=axis) * w[1]
            + lax.slice_in_dim(xp, 2, n + 2, axis=axis) * w[2])


def _diag_scan(bu, a_bar, h0):
    bu = bu.at[:, 0].add(a_bar * h0)
    a = jnp.broadcast_to(a_bar, bu.shape)

    def combine(left, right):
        al, bl = left
        ar, br = right
        return ar * al, ar * bl + br

    _, s = lax.associative_scan(combine, (a, bu), axis=1)
    return s


def _s5(u, p, h0_re, h0_im):
    f32 = jnp.float32
    bsz, L, _ = u.shape
    uc = u.astype(f32).reshape(bsz, L, SSM_GROUPS, SSM_GROUP).astype(jnp.complex64)
    h0 = lax.complex(h0_re.astype(f32), h0_im.astype(f32))
    y = jnp.zeros((bsz, L, SSM_GROUPS, SSM_GROUP), f32)
    fin = []
    for d in range(2):
        a = lax.complex(p['a_re'][d].astype(f32), p['a_im'][d].astype(f32))
        dt = jnp.exp(p['log_dt'][d].astype(f32))[:, None]
        a_bar = jnp.exp(a * dt)
        b_bar = ((a_bar - 1.0) / a)[..., None] * lax.complex(
            p['b_re'][d].astype(f32), p['b_im'][d].astype(f32))
        c_mat = lax.complex(p['c_re'][d].astype(f32), p['c_im'][d].astype(f32))
        ud = uc if d == 0 else jnp.flip(uc, 1)
        bu = jnp.einsum('blgh,gnh->blgn', ud, b_bar)
        s = _diag_scan(bu, a_bar, h0[:, d])
        fin.append(s[:, -1])
        yd = jnp.einsum('blgn,ghn->blgh', s, c_mat).real
        y = y + (yd if d == 0 else jnp.flip(yd, 1))
    y = y.reshape(bsz, L, SSM_WIDTH) + p['d'].astype(f32) * u.astype(f32)
    z = jax.nn.gelu(y).astype(u.dtype)
    out = z * jax.nn.sigmoid(z @ p['w_glu'] + p['b_glu'])
    fin = jnp.stack(fin, axis=1)
    return out, fin.real, fin.imag


def _modulation(cvec, w, b):
    m = jax.nn.silu(cvec) @ w + b
    return jnp.split(m[:, None, :], 6, axis=-1)


def _layer(x, mod, p, rows, h0_re, h0_im):
    sh1, sc1, gt1, sh2, sc2, gt2 = mod
    bsz, L, _ = x.shape
    h = _rms(x, p['g_pre1']) * (1 + sc1) + sh1
    proj = h @ p['w_in']
    gb, gc, hv, u = jnp.split(proj, [CONV_WIDTH, 2 * CONV_WIDTH, 3 * CONV_WIDTH], axis=-1)
    v = gc * hv
    if rows is None:
        v = _dwconv3(v, p['conv_w'], 1)
    else:
        v = _dwconv3(v.reshape(bsz, rows, GRID_W, CONV_WIDTH), p['conv_w'], 2).reshape(bsz, L, CONV_WIDTH)
    conv_out = gb * v
    ssm_out, fin_re, fin_im = _s5(u, p, h0_re, h0_im)
    mixed = jnp.concatenate([_head_rms(conv_out, p['g_conv_out']),
                             _rms(ssm_out, p['g_ssm_out'])], axis=-1) @ p['w_out']
    x = x + gt1 * _rms(mixed, p['g_post1'])
    h = _rms(x, p['g_pre2']) * (1 + sc2) + sh2
    up = h @ p['w_up']
    if rows is None:
        up = _dwconv3(up, p['ffn_conv_w'], 1)
    else:
        up = _dwconv3(up.reshape(bsz, rows, GRID_W, 2 * D_FF), p['ffn_conv_w'], 1).reshape(bsz, L, 2 * D_FF)
    a, g = jnp.split(up, 2, axis=-1)
    ff = (jax.nn.silu(g) * a) @ p['w_down']
    x = x + gt2 * _rms(ff, p['g_post2'])
    return x, fin_re, fin_im


def setup_inputs(seed: int = 0) -> dict:
    key = jax.random.key(seed)
    ks = iter(jax.random.split(key, 40))
    f32 = jnp.float32
    nrm = lambda shape, s: jax.random.normal(next(ks), shape, f32) * s
    gain = lambda shape: 1.0 + 0.02 * jax.random.normal(next(ks), shape, f32)
    n_idx = jnp.arange(SSM_STATE, dtype=f32)
    a_re = -0.5 + nrm((DEPTH, 2, SSM_GROUPS, SSM_STATE), 0.01)
    a_im = math.pi * n_idx + nrm((DEPTH, 2, SSM_GROUPS, SSM_STATE), 0.01)
    log_dt = jax.random.uniform(next(ks), (DEPTH, 2, SSM_GROUPS), f32,
                                math.log(DT_MIN), math.log(DT_MAX))
    return {
        'x_prompt': nrm((BATCH, SEQ, D_MODEL), 1.0),
        'x_sample': nrm((DEC_BATCH, DEC_SEQ, D_MODEL), 1.0),
        'state_ssm_re': nrm((DEC_BATCH, DEPTH, 2, SSM_GROUPS, SSM_STATE), 0.5),
        'state_ssm_im': nrm((DEC_BATCH, DEPTH, 2, SSM_GROUPS, SSM_STATE), 0.5),
        'c': nrm((DEC_BATCH, D_MODEL), 1.0),
        'c_ctx': nrm((D_MODEL,), 1.0),
        'w_ada': nrm((DEPTH, D_MODEL, 6 * D_MODEL), 0.5 * D_MODEL ** -0.5),
        'b_ada': nrm((DEPTH, 6 * D_MODEL), 0.01),
        'g_pre1': gain((DEPTH, D_MODEL)),
        'w_in': nrm((DEPTH, D_MODEL, IN_COLS), D_MODEL ** -0.5),
        'conv_w': nrm((DEPTH, 3, CONV_WIDTH), 3 ** -0.5),
        'ssm_a_re': a_re,
        'ssm_a_im': a_im,
        'ssm_log_dt': log_dt,
        'ssm_b_re': nrm((DEPTH, 2, SSM_GROUPS, SSM_STATE, SSM_GROUP), (2 * SSM_GROUP) ** -0.5),
        'ssm_b_im': nrm((DEPTH, 2, SSM_GROUPS, SSM_STATE, SSM_GROUP), (2 * SSM_GROUP) ** -0.5),
        'ssm_c_re': nrm((DEPTH, 2, SSM_GROUPS, SSM_GROUP, SSM_STATE), (2 * SSM_STATE) ** -0.5),
        'ssm_c_im': nrm((DEPTH, 2, SSM_GROUPS, SSM_GROUP, SSM_STATE), (2 * SSM_STATE) ** -0.5),
        'ssm_d': nrm((DEPTH, SSM_WIDTH), 1.0),
        'w_glu': nrm((DEPTH, SSM_WIDTH, SSM_WIDTH), SSM_WIDTH ** -0.5),
        'b_glu': nrm((DEPTH, SSM_WIDTH), 0.01),
        'g_conv_out': gain((DEPTH, CONV_WIDTH)),
        'g_ssm_out': gain((DEPTH, SSM_WIDTH)),
        'w_out': nrm((DEPTH, MIX_WIDTH, D_MODEL), MIX_WIDTH ** -0.5),
        'g_post1': gain((DEPTH, D_MODEL)),
        'g_pre2': gain((DEPTH, D_MODEL)),
        'w_up': nrm((DEPTH, D_MODEL, 2 * D_FF), D_MODEL ** -0.5),
        'ffn_conv_w': nrm((DEPTH, 3, 2 * D_FF), 3 ** -0.5),
        'w_down': nrm((DEPTH, D_FF, D_MODEL), D_FF ** -0.5),
        'g_post2': gain((DEPTH, D_MODEL)),
    }


def reference(x_prompt, x_sample, state_ssm_re, state_ssm_im, c, c_ctx, w_ada, b_ada,
              g_pre1, w_in, conv_w, ssm_a_re, ssm_a_im, ssm_log_dt, ssm_b_re, ssm_b_im,
              ssm_c_re, ssm_c_im, ssm_d, w_glu, b_glu, g_conv_out, g_ssm_out, w_out,
              g_post1, g_pre2, w_up, ffn_conv_w, w_down, g_post2):
    dec_rows = x_sample.shape[1] // GRID_W
    zero_state = jnp.zeros((x_prompt.shape[0], 2, SSM_GROUPS, SSM_STATE), jnp.float32)
    yp = x_prompt
    ys = x_sample
    new_re = []
    new_im = []
    for l in range(DEPTH):
        p = {
            'g_pre1': g_pre1[l], 'w_in': w_in[l], 'conv_w': conv_w[l],
            'a_re': ssm_a_re[l], 'a_im': ssm_a_im[l], 'log_dt': ssm_log_dt[l],
            'b_re': ssm_b_re[l], 'b_im': ssm_b_im[l], 'c_re': ssm_c_re[l], 'c_im': ssm_c_im[l],
            'd': ssm_d[l], 'w_glu': w_glu[l], 'b_glu': b_glu[l],
            'g_conv_out': g_conv_out[l], 'g_ssm_out': g_ssm_out[l], 'w_out': w_out[l],
            'g_post1': g_post1[l], 'g_pre2': g_pre2[l], 'w_up': w_up[l],
            'ffn_conv_w': ffn_conv_w[l], 'w_down': w_down[l], 'g_post2': g_post2[l],
        }
        mod_ctx = _modulation(c_ctx[None, :], w_ada[l], b_ada[l])
        yp, fin_re, fin_im = _layer(yp, mod_ctx, p, None, zero_state, zero_state)
        new_re.append(fin_re)
        new_im.append(fin_im)
        mod_lat = _modulation(c, w_ada[l], b_ada[l])
        ys, _, _ = _layer(ys, mod_lat, p, dec_rows, state_ssm_re[:, l], state_ssm_im[:, l])
    new_state_re = jnp.stack(new_re, axis=1)
    new_state_im = jnp.stack(new_im, axis=1)
    return (yp, ys, new_state_re, new_state_im)
```

```python
import numpy as np
import concourse.bass as bass
import concourse.mybir as mybir
from concourse.bass_utils import run_bass_kernel_spmd

F32 = mybir.dt.float32
BF16 = mybir.dt.bfloat16
AF = mybir.ActivationFunctionType
ALU = mybir.AluOpType
L = 4
T = 1024
D = 2048
DFF = 5632
EPS = 1e-6
NCH = 257
TWO_PI = 6.283185307179586
MAGIC = 12582912.0


class Tr:
    def __init__(self, nc):
        self.nc = nc
        self.eng = {'pe': nc.tensor, 'act': nc.scalar, 'dve': nc.vector, 'pool': nc.gpsimd, 'sp': nc.sync}
        self.sem = {e: nc.alloc_semaphore('sem_' + e) for e in ('pe', 'act', 'dve', 'pool')}
        self.cnt = {e: 0 for e in self.sem}
        self.waited = {e: {} for e in self.eng}
        self.lastw = {}
        self.readers = {}
        self.dsem = {}

    def _deps(self, reads, writes):
        toks = []
        for r in reads:
            if r in self.lastw:
                toks.append(self.lastw[r])
        for w in writes:
            if w in self.lastw:
                toks.append(self.lastw[w])
            toks += list(self.readers.get(w, {}).values())
        return toks

    def _wait(self, e, toks):
        for (sem, key, val) in toks:
            if self.waited[e].get(key, 0) >= val:
                continue
            self.eng[e].wait_ge(sem, val)
            self.waited[e][key] = val

    def _commit(self, tok, reads, writes):
        for r in reads:
            d = self.readers.setdefault(r, {})
            if d.get(tok[1], (0, 0, 0))[2] < tok[2]:
                d[tok[1]] = tok
        for w in writes:
            self.lastw[w] = tok
            self.readers[w] = {}

    defer = None

    def op(self, e, fn, reads=(), writes=(), inc=True):
        if self.defer is not None:
            self.defer.append(('op', e, fn, list(reads), list(writes), inc))
            return None
        toks = self._deps(reads, writes)
        if e == 'pe':
            toks = [t for t in toks if t[1] != 'pe']
        self._wait(e, toks)
        ins = fn()
        if inc:
            self.cnt[e] += 1
            ins.then_inc(self.sem[e], 1)
            tok = (self.sem[e], e, self.cnt[e])
            self._commit(tok, reads, writes)
        return ins

    def dma(self, q, out, in_, reads, writes, key):
        if self.defer is not None:
            self.defer.append(('dma', q, out, in_, list(reads), list(writes), key))
            return
        toks = self._deps(reads, writes)
        self._wait(q, toks)
        if key not in self.dsem:
            self.dsem[key] = [self.nc.alloc_semaphore('d_' + key), 0]
        ent = self.dsem[key]
        ins = self.eng[q].dma_start(out=out, in_=in_)
        ent[1] += 16
        ins.then_inc(ent[0], 16)
        tok = (ent[0], 'dma_' + key, ent[1])
        self._commit(tok, reads, writes)

    def emit_interleaved(self, chains):
        assert self.defer is None
        n = max(len(c) for c in chains)
        pos = [0] * len(chains)
        for step in range(1, n + 1):
            for ci, c in enumerate(chains):
                tgt = (len(c) * step + n - 1) // n
                while pos[ci] < min(tgt, len(c)):
                    it = c[pos[ci]]
                    pos[ci] += 1
                    if it[0] == 'op':
                        self.op(*it[1:])
                    else:
                        self.dma(*it[1:])

    def finish(self):
        for key, (sem, val) in self.dsem.items():
            self.eng['sp'].wait_ge(sem, val)
        for e in ('pe', 'act', 'dve', 'pool'):
            if self.cnt[e]:
                self.eng['sp'].wait_ge(self.sem[e], self.cnt[e])


STOP = [99]
P4LIST = [[0, 1, 2, 3]]


def build_nc():
    nc = bass.Bass("TRN2", target_bir_lowering=False)
    tr = Tr(nc)

    def din(name, shape, dt=F32):
        return nc.dram_tensor(name, list(shape), dt, kind="ExternalInput").ap()

    x0 = din("x0", [128, 16, T])
    y = nc.dram_tensor("y", [128, 16, T], F32, kind="ExternalOutput").ap()
    fin = nc.dram_tensor("fin", [128, L * 8 * 64], F32, kind="ExternalOutput").ap()
    dbg = nc.dram_tensor("dbg", [128, 16, T], F32, kind="ExternalOutput").ap() if STOP[0] != 99 else None
    cvec = din("cvec", [128, 16])
    flags = din("flags", [128, 4])
    cfix = din("cfix", [128, 16])
    cmask = din("cmask", [128, 2 * 128])
    rmask = din("rmask", [128, 129])
    w_ada = din("w_ada", [L, D, 6 * D])
    w_in = din("w_in", [L, D, 4096])
    w_glu = din("w_glu", [L, 1024, 1024])
    w_out = din("w_out", [L, D, D])
    w_up = din("w_up", [L, D, 2 * DFF])
    w_down = din("w_down", [L, DFF, D])
    pk16 = din("pk16", [128, L * 4 * 16])
    bada = din("bada", [128, L * 96])
    pk8 = din("pk8", [128, L * 4 * 8])
    cw = din("cw", [128, L * 3 * 8])
    fcw = din("fcw", [128, L * 3 * 88])
    parR = din("parR", [L * 8, 128, 5 * 256])
    parCC = din("parCC", [L * 8, 128, 5 * 256])
    parH = din("parH", [128, L * 8 * 16])

    arena = nc.alloc_sbuf_tensor("arena", [128, 212000], mybir.dt.uint8)
    off = [16512]

    offs = {}

    def at(name, shape, dt, o=None):
        nbytes = int(np.prod(shape[1:])) * (2 if dt == BF16 else 4)
        if o is None:
            o = off[0]
            offs[name] = o
            off[0] += (nbytes + 31) // 32 * 32
            assert off[0] <= 16512 + 211900, (name, off[0])
        return nc.alloc_sbuf_tensor_at(name, list(shape), dt, offset=o)

    H = at("H", [128, 16, T], BF16)
    MX = at("MX", [128, 16, T], BF16)
    a0 = off[0]
    ACTB = at("ACTB", [128, 44, T], BF16)
    Z = at("Z", [128, 8, T], BF16, a0)
    SSMO = at("SSMO", [128, 8, T], BF16, a0 + 16384)
    MIXED = at("MIXED", [128, 16, T], BF16, a0 + 32768)
    S5S = at("S5S", [128, 32, 256], F32, a0 + 32768)
    ST = at("ST", [128, 2, T], BF16, a0 + 65536)
    WIN = at("WIN", [128, 2, 8, 2, 128], BF16, a0 + 69632)
    WOUT = at("WOUT", [128, 2, 9, 2, 5, 32], BF16, a0 + 77824)
    assert 77824 + 11520 <= 90112
    WT = at("WT", [128, 3, 16, 128], BF16)
    XT = at("XT", [128, 2, T], F32)
    TMP = at("TMP", [128, 3, T], F32)
    RSTD = at("RSTD", [128, T], F32)
    SQ = at("SQ", [128, T], BF16)
    CHB = at("CHB", [128, 2, 2, 128], BF16, a0 + 16384)
    h0_ = 16512
    ST2 = at("ST2", [128, 2, 2, T], BF16, h0_)
    TR = at("TR", [128, 8, 129], F32, h0_ + 8192)
    TI = at("TI", [128, 8, 129], F32, h0_ + 12320)
    DZ = at("DZ", [128, 8, 129], F32, h0_ + 16448)
    TA = at("TA", [128, 8, 129], F32, h0_ + 20576)
    TB = at("TB", [128, 8, 129], F32, h0_ + 24704)
    XR = at("XR", [128, 2, 129], F32, h0_ + 28832)
    XI = at("XI", [128, 2, 129], F32, h0_ + 28832 + 1056)
    XHR = at("XHR", [128, 2, 129], F32, a0 + 16384 + 1024)
    XHI = at("XHI", [128, 2, 129], F32, a0 + 16384 + 1024 + 1056)
    SHR = at("SHR", [128, 2, 129], F32, a0 + 16384 + 1024 + 2112)
    SHI = at("SHI", [128, 2, 129], F32, a0 + 16384 + 1024 + 3168)
    assert 28832 + 2112 <= 32768
    SM = at("SM", [128, 64, 8], F32, a0 + 16384 + 13408)
    FINS = at("FINS", [128, 64], F32, a0 + 16384 + 15456)
    D0 = at("D0", [128, 512], BF16)
    ONES = at("ONES", [128, 128], BF16)
    BONES = at("BONES", [128, 128], BF16)
    MOD2 = at("MOD2", [128, 2, 96], F32)
    BADA = at("BADA", [128, 96], F32)
    PK16 = at("PK16", [128, L, 4, 16], F32)
    PK8 = at("PK8", [128, L, 4, 8], F32)
    CW = at("CW", [128, L, 3, 8], F32)
    FCW = at("FCW", [128, 3, 88], F32)
    FCD = at("FCD", [128, 8, 88], F32)
    CWD = at("CWD", [128, 2, 8], F32)
    CV = at("CV", [128, 16], F32)
    SCB = at("SCB", [128, 16], BF16)
    FLG = at("FLG", [128, 4], F32)
    CFX = at("CFX", [128, 16], F32)
    COEF = at("COEF", [128, 4, 16], F32)
    EPSV = at("EPSV", [128, 4], F32)
    FX = at("FX", [128, 4, 16], F32)
    PH = at("PH", [128, L * 8 * 16], F32)
    U3 = at("U3", [128, T], BF16)
    MASKP = at("MASKP", [128, 8], F32)
    CMASK = at("CMASK", [128, 2, 128], F32)
    RM = at("RM", [128, 129], F32)

    ST2b = at("ST2b", [128, 2, 2, T], BF16, offs["XT"])
    o_st = a0 + 65536
    XRb = at("XRb", [128, 2, 129], F32, o_st)
    XIb = at("XIb", [128, 2, 129], F32, o_st + 1056)
    XHRb = at("XHRb", [128, 2, 129], F32, o_st + 2112)
    TMPB = at("TMPB", [128, 128], F32, o_st + 3168)
    XHIb = at("XHIb", [128, 2, 129], F32, offs["RSTD"])
    SHRb = at("SHRb", [128, 2, 129], F32, offs["RSTD"] + 1056)
    SHIb = at("SHIb", [128, 2, 129], F32, offs["RSTD"] + 2112)
    CHBb = at("CHBb", [128, 2, 2, 128], BF16, offs["SQ"])
    ST2P = [ST2, ST2b]
    CHBP = [CHB, CHBb]
    XS = [(XR, XI, XHR, XHI, SHR, SHI), (XRb, XIb, XHRb, XHIb, SHRb, SHIb)]

    PS = nc.alloc_psum_tensor("PS", [128, 8, 512], F32)

    def psb(i):
        return PS[:, 2 * i:2 * i + 2, :].rearrange("p a b -> p (a b)")

    V, A_, P_, S_ = 'dve', 'act', 'pe', 'sp'

    EH = {'dve': nc.vector, 'pool': nc.gpsimd}
    G = 'dve'

    def tt(out, a, b, op, r, w, e=V):
        tr.op(e, lambda: EH[e].tensor_tensor(out=out, in0=a, in1=b, op=op), r, w)

    def stt(out, a, sc, b, op0, op1, r, w):
        tr.op(V, lambda: nc.vector.scalar_tensor_tensor(out=out, in0=a, scalar=sc, in1=b, op0=op0, op1=op1), r, w)

    def ts(out, a, s1, s2, op0, op1, r, w, e=V):
        if s2 is None:
            tr.op(e, lambda: EH[e].tensor_scalar(out=out, in0=a, scalar1=s1, scalar2=None, op0=op0), r, w)
        else:
            tr.op(e, lambda: EH[e].tensor_scalar(out=out, in0=a, scalar1=s1, scalar2=s2, op0=op0, op1=op1), r, w)

    def act(out, in_, func, r, w, bias=None, scale=None):
        kw = {}
        if bias is not None:
            kw['bias'] = bias
        if scale is not None:
            kw['scale'] = scale
        tr.op(A_, lambda: nc.scalar.activation(out=out, in_=in_, func=func, **kw), r, w)

    def vcopy(out, in_, r, w, e=V):
        tr.op(e, lambda: EH[e].tensor_copy(out=out, in_=in_), r, w)

    def vset(ap, c, w):
        tr.op(V, lambda: nc.vector.memset(ap, c), (), w)

    def scan(out, d0ap, d1ap, r, w):
        tr.op(V, lambda: nc.vector.tensor_tensor_scan(out=out, data0=d0ap, data1=d1ap, initial=0.0,
                                                      op0=ALU.mult, op1=ALU.add), r, w)

    def recip(out, in_, r, w):
        tr.op(V, lambda: nc.vector.reciprocal(out=out, in_=in_), r, w)

    def mm(out, lhsT, rhs, start, stop, r, w, inc, **kw):
        tr.op(P_, lambda: nc.tensor.matmul(out, lhsT=lhsT, rhs=rhs, start=start, stop=stop, **kw), r, w, inc=inc)

    def ld(out, in_, w, key, q=S_, r=()):
        tr.dma(q, out, in_, list(r), list(w), key)

    def dump(src, n, keys, off_=0):
        for kc in range(n):
            act(TMP[:, 0, :], src(kc), AF.Copy, keys, ['T0'])
            ld(dbg[:, off_ + kc, :], TMP[:, 0, :], ['DBG%d' % (off_ + kc)], 'dbg', r=['T0'])

    ld(CV[:], cvec, ['CV'], 'c0')
    ld(FLG[:], flags, ['FLG'], 'c1')
    ld(CFX[:], cfix, ['CFX'], 'c2')
    ld(PK16[:].rearrange("p a b c -> p (a b c)"), pk16, ['PK16'], 'c3')
    ld(PK8[:].rearrange("p a b c -> p (a b c)"), pk8, ['PK8'], 'c4')
    ld(CW[:].rearrange("p a b c -> p (a b c)"), cw, ['CW'], 'c5')
    ld(PH[:], parH, ['PH'], 'c6')
    ld(CMASK[:].rearrange("p a b -> p (a b)"), cmask, ['CMASK'], 'c10')
    ld(RM[:], rmask, ['RM'], 'c11')
    vset(ONES[:], 1.0, ['ONES'])
    vset(BONES[:], 0.0, ['BONES'])
    vset(BONES[0:64, 0:64], 1.0, ['BONES'])
    vset(BONES[64:128, 64:128], 1.0, ['BONES'])
    vset(D0[:], 1.0, ['D0'])
    vset(D0[:].rearrange("p (c j) -> p c j", j=8)[:, :, 0:1], 0.0, ['D0'])
    vset(MASKP[64:128, :], 1.0, ['MASKP'])
    vset(MASKP[64:96, :], 0.0, ['MASKP'])
    vset(U3[:], 0.0, ['U3'])
    vset(EPSV[:, 0:1], EPS, ['EPSV'])
    vset(EPSV[:, 1:2], 0.0, ['EPSV'])
    act(TMP[:, 0, 0:16], CV[:], AF.Sigmoid, ['CV'], ['T0'])
    tt(SCB[:], TMP[:, 0, 0:16], CV[:], ALU.mult, ['T0', 'CV'], ['SCB'])

    wl = []

    def wtile(w, l, kc0, nkc, oc):
        v = w[l].rearrange("(kc p) n -> p kc n", p=128)
        return (v[:, kc0:kc0 + nkc, oc * 128:(oc + 1) * 128], nkc)

    for jc in range(96):
        wl.append(wtile(w_ada, 0, 0, 16, jc))
    for l in range(L):
        for oc in list(range(24, 32)) + [x for c in range(8) for x in (8 + c, 16 + c, c)]:
            wl.append(wtile(w_in, l, 0, 16, oc))
        if l + 1 < L:
            for jc in range(96):
                wl.append(wtile(w_ada, l + 1, 0, 16, jc))
        for oc in range(8):
            wl.append(wtile(w_glu, l, 0, 8, oc))
        for oc in range(16):
            wl.append(wtile(w_out, l, 0, 16, oc))
        for fc in range(44):
            wl.append(wtile(w_up, l, 0, 16, fc))
            wl.append(wtile(w_up, l, 0, 16, 44 + fc))
        for oc in range(16):
            for (k0, nk) in ((0, 16), (16, 16), (32, 12)):
                wl.append(wtile(w_down, l, k0, nk, oc))
    wstate = {'issued': 0, 'used': 0}

    def wissue():
        i = wstate['issued']
        if i >= len(wl):
            return
        ap, nkc = wl[i]
        s = i % 3
        ld(WT[:, s, 0:nkc, :], ap, ['WT%d' % s], 'w%d' % s, q='pool')
        wstate['issued'] += 1

    def wget():
        i = wstate['used']
        while wstate['issued'] < min(i + 3, len(wl)):
            wissue()
        wstate['used'] += 1
        s = i % 3
        return WT[:, s], 'WT%d' % s

    def wprefetch():
        while wstate['issued'] < min(wstate['used'] + 3, len(wl)):
            wissue()

    xsrc = {'ap': x0}

    def xload(kc, slot):
        ld(XT[:, slot, :], xsrc['ap'][:, kc, :], ['XT%d' % slot], 'x%d' % slot, r=['Y%d' % kc])

    def rstd_from(psi, r):
        act(RSTD[:], psb(psi), AF.Sqrt, ['PS%d' % psi, 'EPSV'], ['RSTD'], bias=EPSV[:, 0:1], scale=1.0 / r)
        recip(RSTD[:], RSTD[:], ['RSTD'], ['RSTD'])

    def norm_to_H(avec, shvec):
        for kc in range(16):
            xload(kc, kc % 2)
            act(SQ[:], XT[:, kc % 2, :], AF.Square, ['XT%d' % (kc % 2)], ['SQ'])
            for b in range(2):
                mm(PS[:, 4 + b, :], ONES[:], SQ[:, b * 512:(b + 1) * 512], kc == 0, kc == 15,
                   ['SQ', 'ONES'], ['PS2'], inc=(b == 1))
        rstd_from(2, D)
        for kc in range(16):
            xload(kc, kc % 2)
            stt(TMP[:, kc % 2, :], XT[:, kc % 2, :], avec[:, kc:kc + 1], RSTD[:], ALU.mult, ALU.mult,
                ['XT%d' % (kc % 2), 'RSTD', 'COEF'], ['T%d' % (kc % 2)])
            act(H[:, kc, :], TMP[:, kc % 2, :], AF.Identity, ['T%d' % (kc % 2), 'MOD'], ['H%d' % kc],
                bias=shvec[:, kc:kc + 1], scale=1.0)

    def residual(SRC, srckey, cvecap):
        for kc in range(16):
            act(SQ[:], SRC[:, kc, :], AF.Square, ['%s%d' % (srckey, kc)], ['SQ'])
            for b in range(2):
                mm(PS[:, 4 + b, :], ONES[:], SQ[:, b * 512:(b + 1) * 512], kc == 0, kc == 15,
                   ['SQ', 'ONES'], ['PS2'], inc=(b == 1))
        rstd_from(2, D)
        for kc in range(16):
            s = kc % 2
            xload(kc, s)
            stt(TMP[:, s, :], SRC[:, kc, :], cvecap[:, kc:kc + 1], RSTD[:], ALU.mult, ALU.mult,
                ['%s%d' % (srckey, kc), 'RSTD', 'COEF'], ['T%d' % s])
            tt(TMP[:, s, :], XT[:, s, :], TMP[:, s, :], ALU.add, ['XT%d' % s, 'T%d' % s], ['T%d' % s])
            ld(y[:, kc, :], TMP[:, s, :], ['Y%d' % kc], 'yo%d' % s, r=['T%d' % s])
        xsrc['ap'] = y

    def big_mm(psi, wt, wkey, rhs, rkeys, nkc, first=True, last=True):
        for b in range(2):
            for kc in range(nkc):
                mm(PS[:, 2 * psi + b, :], wt[:, kc, :], rhs(kc)[:, b * 512:(b + 1) * 512],
                   first and kc == 0, last and kc == nkc - 1,
                   [wkey] + rkeys, ['PS%d' % psi], inc=(b == 1 and kc == nkc - 1))

    Hkeys = ['H%d' % k for k in range(16)]

    for l in range(L):
        def emit_mods(jcs):
            for jc in jcs:
                wt_, wk_ = wget()
                for kc in range(16):
                    mm(PS[:, 6, jc:jc + 1], wt_[:, kc, :], SCB[:, kc:kc + 1], kc == 0, kc == 15,
                       [wk_, 'SCB'], ['PS3'], inc=(kc == 15))

        MOD = MOD2[:, l % 2, :]
        ld(BADA[:], bada[:, l * 96:(l + 1) * 96], ['BADA'], 'c8')
        ld(FCW[:].rearrange("p a b -> p (a b)"), fcw[:, l * 264:(l + 1) * 264], ['FCW'], 'c9')
        if l == 0:
            emit_mods(range(96))
        tt(MOD, PS[:, 6, 0:96], BADA[:], ALU.add, ['PS3', 'BADA'], ['MOD'])
        sh1, sc1, gt1, sh2, sc2, gt2 = [MOD[:, i * 16:(i + 1) * 16] for i in range(6)]
        for i, (g, sc) in enumerate(((0, sc1), (2, sc2))):
            stt(COEF[:, 2 * i, :], sc, 1.0, PK16[:, l, g, :], ALU.add, ALU.mult, ['MOD', 'PK16'], ['COEF'])
        for i, (g, gt) in enumerate(((1, gt1), (3, gt2))):
            tt(COEF[:, 2 * i + 1, :], gt, PK16[:, l, g, :], ALU.mult, ['MOD', 'PK16'], ['COEF'])
        ts(CWD[:, 0, :], CW[:, l, 0, :], -1.0, None, ALU.mult, None, ['CW'], ['CWD'])
        ts(CWD[:, 1, :], CW[:, l, 2, :], -1.0, None, ALU.mult, None, ['CW'], ['CWD'])
        ts(FCD[:, 0, :], FCW[:, 0, :], FLG[:, 1:2], None, ALU.mult, None, ['FCW', 'FLG'], ['FCD'])
        ts(FCD[:, 1, :], FCW[:, 2, :], FLG[:, 1:2], None, ALU.mult, None, ['FCW', 'FLG'], ['FCD'])
        ts(FCD[:, 2, :], FCW[:, 0, :], FLG[:, 2:3], None, ALU.mult, None, ['FCW', 'FLG'], ['FCD'])
        ts(FCD[:, 3, :], FCW[:, 2, :], FLG[:, 2:3], None, ALU.mult, None, ['FCW', 'FLG'], ['FCD'])
        ts(FCD[:, 4, :], FCD[:, 0, :], -1.0, None, ALU.mult, None, ['FCD'], ['FCD'])
        ts(FCD[:, 5, :], FCD[:, 1, :], -1.0, None, ALU.mult, None, ['FCD'], ['FCD'])

        if STOP[0] == 0:
            tr.finish()
            return nc
        norm_to_H(COEF[:, 0, :], sh1)
        if STOP[0] == 1:
            dump(lambda kc: H[:, kc, :], 16, Hkeys)
            tr.finish()
            return nc
        for ut in range(8):
            wt, wk = wget()
            big_mm(ut % 2, wt, wk, lambda kc: H[:, kc, :], Hkeys, 16)
            act(MX[:, 8 + ut, :], psb(ut % 2), AF.Copy, ['PS%d' % (ut % 2)], ['MX%d' % (8 + ut)])
        for c in range(8):
            wt, wk = wget()
            big_mm(0, wt, wk, lambda kc: H[:, kc, :], Hkeys, 16)
            act(TMP[:, 0, :], psb(0), AF.Copy, ['PS0'], ['T0'])
            wt, wk = wget()
            big_mm(1, wt, wk, lambda kc: H[:, kc, :], Hkeys, 16)
            tt(TMP[:, 1, :], psb(1), TMP[:, 0, :], ALU.mult, ['PS1', 'T0'], ['T1'])
            act(TMP[:, 2, :], TMP[:, 1, :], AF.Copy, ['T1', 'CW'], ['T2'], scale=CW[:, l, 1, c:c + 1])
            stt(TMP[:, 2, 1:T], TMP[:, 1, 0:T - 1], CW[:, l, 0, c:c + 1], TMP[:, 2, 1:T], ALU.mult, ALU.add,
                ['T1', 'T2', 'CW'], ['T2'])
            stt(TMP[:, 2, 0:T - 1], TMP[:, 1, 1:T], CW[:, l, 2, c:c + 1], TMP[:, 2, 0:T - 1], ALU.mult, ALU.add,
                ['T1', 'T2', 'CW'], ['T2'])
            v64 = TMP[:, 1, :].rearrange("p (r c) -> p r c", c=64)
            o64 = TMP[:, 2, :].rearrange("p (r c) -> p r c", c=64)
            tt(FX[:, 0, 0:15], v64[:, 0:15, 63], CFX[:, 0:15], ALU.mult, ['T1', 'CFX'], ['FX'])
            stt(o64[:, 1:16, 0], FX[:, 0, 0:15], CWD[:, 0, c:c + 1], o64[:, 1:16, 0], ALU.mult, ALU.add,
                ['FX', 'T2', 'CWD'], ['T2'])
            tt(FX[:, 1, 0:15], v64[:, 1:16, 0], CFX[:, 0:15], ALU.mult, ['T1', 'CFX'], ['FX'])
            stt(o64[:, 0:15, 63], FX[:, 1, 0:15], CWD[:, 1, c:c + 1], o64[:, 0:15, 63], ALU.mult, ALU.add,
                ['FX', 'T2', 'CWD'], ['T2'])
            wt, wk = wget()
            big_mm(0, wt, wk, lambda kc: H[:, kc, :], Hkeys, 16)
            tt(TMP[:, 0, :], psb(0), TMP[:, 2, :], ALU.mult, ['PS0', 'T2'], ['T0'])
            act(SQ[:], TMP[:, 0, :], AF.Square, ['T0'], ['SQ'])
            for b in range(2):
                mm(PS[:, 4 + b, :], BONES[:], SQ[:, b * 512:(b + 1) * 512], True, True, ['SQ', 'BONES'], ['PS2'],
                   inc=(b == 1))
            rstd_from(2, 64)
            stt(MX[:, c, :], TMP[:, 0, :], PK8[:, l, 2, c:c + 1], RSTD[:], ALU.mult, ALU.mult,
                ['T0', 'RSTD', 'PK8'], ['MX%d' % c])

        if STOP[0] == 2:
            dump(lambda kc: MX[:, kc, :], 16, ['MX%d' % k for k in range(16)])
            tr.finish()
            return nc
        SA = lambda i: S5S[:, i, :]
        SK = lambda i: 'S%d' % i

        def cmul(orr, oi, ar, ai, br, bi, t1, t2, e=V):
            tt(SA(t1), SA(ar), SA(br), ALU.mult, [SK(ar), SK(br)], [SK(t1)], e=e)
            tt(SA(t2), SA(ai), SA(bi), ALU.mult, [SK(ai), SK(bi)], [SK(t2)], e=e)
            tt(SA(t1), SA(t1), SA(t2), ALU.subtract, [SK(t1), SK(t2)], [SK(t1)], e=e)
            tt(SA(t2), SA(ar), SA(bi), ALU.mult, [SK(ar), SK(bi)], [SK(t2)], e=e)
            tt(SA(oi), SA(ai), SA(br), ALU.mult, [SK(ai), SK(br)], [SK(oi)], e=e)
            tt(SA(oi), SA(oi), SA(t2), ALU.add, [SK(oi), SK(t2)], [SK(oi)], e=e)
            vcopy(SA(orr), SA(t1), [SK(t1)], [SK(orr)], e=e)

        def sincos(li, so, co, t1, e=V):
            for (dst, shift) in ((so, 0.0), (co, np.pi / 2)):
                ts(SA(t1), SA(li), shift, 1.0 / TWO_PI, ALU.add, ALU.mult, [SK(li)], [SK(t1)], e=e)
                ts(SA(t1), SA(t1), MAGIC, None, ALU.add, None, [SK(t1)], [SK(t1)], e=e)
                ts(SA(t1), SA(t1), -MAGIC, None, ALU.add, None, [SK(t1)], [SK(t1)], e=e)
                ts(SA(t1), SA(t1), -TWO_PI, None, ALU.mult, None, [SK(t1)], [SK(t1)], e=e)
                tt(SA(t1), SA(t1), SA(li), ALU.add, [SK(t1), SK(li)], [SK(t1)], e=e)
                ts(SA(t1), SA(t1), shift, None, ALU.add, None, [SK(t1)], [SK(t1)], e=e)
                ts(SA(t1), SA(t1), 3.1415925, -3.1415925, ALU.min, ALU.max, [SK(t1)], [SK(t1)], e=e)
                act(SA(dst), SA(t1), AF.Sin, [SK(t1)], [SK(dst)])

        def abar_from(base, o_r, o_i, t, e=V):
            act(SA(t), SA(base + 2), AF.Exp, [SK(base + 2)], [SK(t)])
            tt(SA(t + 1), SA(base), SA(t), ALU.mult, [SK(base), SK(t)], [SK(t + 1)], e=e)
            tt(SA(t + 2), SA(base + 1), SA(t), ALU.mult, [SK(base + 1), SK(t)], [SK(t + 2)], e=e)
            act(SA(t + 1), SA(t + 1), AF.Exp, [SK(t + 1)], [SK(t + 1)])
            sincos(t + 2, o_i, o_r, t + 3, e=e)
            tt(SA(o_r), SA(o_r), SA(t + 1), ALU.mult, [SK(o_r), SK(t + 1)], [SK(o_r)], e=e)
            tt(SA(o_i), SA(o_i), SA(t + 1), ALU.mult, [SK(o_i), SK(t + 1)], [SK(o_i)], e=e)

        ack = ['AC%d' % k for k in range(8, 16)]
        vset(CHB[:].rearrange("p a b c -> p (a b c)"), 0.0, ['CHB'] + ack)
        vset(SM[:].rearrange("p a b -> p (a b)"), 0.0, ['SM'] + ack)
        vset(FINS[:], 0.0, ['FINS'] + ack)
        for d_ in range(2):
            vset(WOUT[:, d_, :, :, 3, :], 0.0, ['WOUT'])
        for ut in range(8):
            lu = l * 8 + ut
            tr.defer = []
            ld(S5S[:, 0:5, :].rearrange("p a b -> p (a b)"), parR[lu], [SK(i) for i in range(5)], 'pr')
            abar_from(0, 5, 6, 7, e=G)
            tt(SA(7), SA(0), SA(0), ALU.mult, [SK(0)], [SK(7)], e=G)
            tt(SA(8), SA(1), SA(1), ALU.mult, [SK(1)], [SK(8)], e=G)
            tt(SA(7), SA(7), SA(8), ALU.add, [SK(7), SK(8)], [SK(7)], e=G)
            recip(SA(7), SA(7), [SK(7)], [SK(7)])
            ts(SA(8), SA(5), -1.0, None, ALU.add, None, [SK(5)], [SK(8)], e=G)
            tt(SA(9), SA(8), SA(0), ALU.mult, [SK(8), SK(0)], [SK(9)], e=G)
            tt(SA(10), SA(6), SA(1), ALU.mult, [SK(6), SK(1)], [SK(10)], e=G)
            tt(SA(9), SA(9), SA(10), ALU.add, [SK(9), SK(10)], [SK(9)], e=G)
            tt(SA(9), SA(9), SA(7), ALU.mult, [SK(9), SK(7)], [SK(9)], e=G)
            tt(SA(10), SA(6), SA(0), ALU.mult, [SK(6), SK(0)], [SK(10)], e=G)
            tt(SA(11), SA(8), SA(1), ALU.mult, [SK(8), SK(1)], [SK(11)], e=G)
            tt(SA(10), SA(10), SA(11), ALU.subtract, [SK(10), SK(11)], [SK(10)], e=G)
            tt(SA(10), SA(10), SA(7), ALU.mult, [SK(10), SK(7)], [SK(10)], e=G)
            cmul(11, 12, 9, 10, 3, 4, 13, 14, e=G)
            tt(SA(13), SA(5), SA(5), ALU.mult, [SK(5)], [SK(13)], e=G)
            tt(SA(14), SA(6), SA(6), ALU.mult, [SK(6)], [SK(14)], e=G)
            tt(SA(13), SA(13), SA(14), ALU.add, [SK(13), SK(14)], [SK(13)], e=G)
            recip(SA(13), SA(13), [SK(13)], [SK(13)])
            tt(SA(7), SA(5), SA(13), ALU.mult, [SK(5), SK(13)], [SK(7)], e=G)
            ts(SA(8), SA(6), -1.0, None, ALU.mult, None, [SK(6)], [SK(8)], e=G)
            tt(SA(8), SA(8), SA(13), ALU.mult, [SK(8), SK(13)], [SK(8)], e=G)
            for j in range(8):
                for d in range(2):
                    vcopy(WIN[:, d, j, 0, :], S5S[:, 11, d * 128:(d + 1) * 128], [SK(11)], ['WIN'], e=G)
                    vcopy(WIN[:, d, j, 1, :], S5S[:, 12, d * 128:(d + 1) * 128], [SK(12)], ['WIN'], e=G)
                if j < 7:
                    cmul(11, 12, 11, 12, 7, 8, 13, 14, e=G)
            chain_r, tr.defer = tr.defer, []
            ld(S5S[:, 16:21, :].rearrange("p a b -> p (a b)"), parCC[lu], [SK(i) for i in range(16, 21)], 'pc')
            abar_from(16, 21, 22, 23)
            cc = lambda i: S5S[:, i, :].rearrange("p (d q h) -> p d q h", d=2, q=4)
            sm = lambda i: SM[:, i, :].rearrange("p (d q) -> p d q", d=2)
            for j in range(9):
                vcopy(WOUT[:, :, j, 0, 0:3, :], cc(19)[:, :, 0:3, :], [SK(19)], ['WOUT'])
                vcopy(WOUT[:, :, j, 0, 4, :], cc(19)[:, :, 3, :], [SK(19)], ['WOUT'])
                ts(WOUT[:, :, j, 1, 0:3, :], cc(20)[:, :, 0:3, :], -1.0, None, ALU.mult, None, [SK(20)], ['WOUT'])
                ts(WOUT[:, :, j, 1, 4, :], cc(20)[:, :, 3, :], -1.0, None, ALU.mult, None, [SK(20)], ['WOUT'])
                if j < 8:
                    cmul(19, 20, 19, 20, 21, 22, 25, 26)
                if j == 6:
                    pass
            vcopy(sm(0), cc(21)[:, :, :, 0], [SK(21)], ['SM'])
            vcopy(sm(1), cc(22)[:, :, :, 0], [SK(22)], ['SM'])

            def smul(orr, oi, ar, ai, br, bi):
                tt(SM[:, 60, :], SM[:, ar, :], SM[:, br, :], ALU.mult, ['SM'], ['SM'])
                tt(SM[:, 61, :], SM[:, ai, :], SM[:, bi, :], ALU.mult, ['SM'], ['SM'])
                tt(SM[:, 60, :], SM[:, 60, :], SM[:, 61, :], ALU.subtract, ['SM'], ['SM'])
                tt(SM[:, 61, :], SM[:, ar, :], SM[:, bi, :], ALU.mult, ['SM'], ['SM'])
                tt(SM[:, 62, :], SM[:, ai, :], SM[:, br, :], ALU.mult, ['SM'], ['SM'])
                tt(SM[:, oi, :], SM[:, 61, :], SM[:, 62, :], ALU.add, ['SM'], ['SM'])
                vcopy(SM[:, orr, :], SM[:, 60, :], ['SM'], ['SM'])

            vcopy(SM[:, 2, :], SM[:, 0, :], ['SM'], ['SM'])
            vcopy(SM[:, 3, :], SM[:, 1, :], ['SM'], ['SM'])
            for _ in range(6):
                smul(2, 3, 2, 3, 0, 1)
            smul(4, 5, 2, 3, 0, 1)
            tt(SM[:, 60, :], SM[:, 4, :], SM[:, 4, :], ALU.mult, ['SM'], ['SM'])
            tt(SM[:, 61, :], SM[:, 5, :], SM[:, 5, :], ALU.mult, ['SM'], ['SM'])
            tt(SM[:, 60, :], SM[:, 60, :], SM[:, 61, :], ALU.add, ['SM'], ['SM'])
            act(SM[:, 30, :], SM[:, 60, :], AF.Sqrt, ['SM'], ['SM'])
            recip(SM[:, 31, :], SM[:, 30, :], ['SM'], ['SM'])
            tt(SM[:, 6, :], SM[:, 4, :], SM[:, 31, :], ALU.mult, ['SM'], ['SM'])
            tt(SM[:, 7, :], SM[:, 5, :], SM[:, 31, :], ALU.mult, ['SM'], ['SM'])
            for k in range(1, 8):
                smul(6 + 2 * k, 7 + 2 * k, 4 + 2 * k, 5 + 2 * k, 4 + 2 * k, 5 + 2 * k)
            v4 = lambda X: X[:].rearrange("p (q d) i -> p q d i", d=2)
            cfv = lambda idx, n: SM[:, idx, :].rearrange("p (d q) -> p q d", d=2).unsqueeze(3).to_broadcast(
                [128, 4, 2, n])
            hk = Hkeys if ut == 0 else []
            vset(TR[:, :, 0:1], 1.0, ['TR'] + hk)
            vset(TI[:, :, 0:1], 0.0, ['TI'] + hk)
            for k in range(8):
                n = (1 << k) if k < 7 else 1
                s0, s1, d0_, d1_ = (0, n, n, 2 * n) if k < 7 else (0, 1, 128, 129)
                ur, ui = cfv(6 + 2 * k, n), cfv(7 + 2 * k, n)
                ta, tb = v4(TA)[:, :, :, 0:n], v4(TB)[:, :, :, 0:n]
                tt(ta, v4(TR)[:, :, :, s0:s1], ur, ALU.mult, ['TR', 'SM'], ['TA'])
                tt(tb, v4(TI)[:, :, :, s0:s1], ui, ALU.mult, ['TI', 'SM'], ['TB'])
                tt(v4(TR)[:, :, :, d0_:d1_], ta, tb, ALU.subtract, ['TA', 'TB'], ['TR'])
                tt(ta, v4(TR)[:, :, :, s0:s1], ui, ALU.mult, ['TR', 'SM'], ['TA'])
                tt(tb, v4(TI)[:, :, :, s0:s1], ur, ALU.mult, ['TI', 'SM'], ['TB'])
                tt(v4(TI)[:, :, :, d0_:d1_], ta, tb, ALU.add, ['TA', 'TB'], ['TI'])
            tt(v4(DZ), cfv(30, 129), RM[:].rearrange("p (a b i) -> p a b i", a=1, b=1).to_broadcast([128, 4, 2, 129]),
               ALU.mult, ['SM', 'RM'], ['DZ'] + hk)
            chain_c, tr.defer = tr.defer, None
            tr.emit_interleaved([chain_r, chain_c])
            if STOP[0] == 3:
                tr.finish()
                return nc

            ts(U3[64:128, :], MX[64:128, 8 + ut, :], MASKP[64:128, 0:1], None, ALU.mult, None,
               ['MX%d' % (8 + ut), 'MASKP'], ['U3'])

            def stage_a(P4):
                par = P4 % 2
                ST2_, (XR_, XI_) = ST2P[par], XS[par][0:2]
                kx, kst = 'X%d' % par, 'ST%d' % par
                hka = hk + (['XT0', 'XT1', 'RSTD', 'SQ'] if ut == 0 else [])
                if P4 < 3:
                    urows = MX[32 * P4:32 * P4 + 32, 8 + ut, :]
                    wi = lambda d, jx, rx: WIN[32 * P4:32 * P4 + 32, d, jx, rx, :]
                    tpi, ukey = (32 * P4, 0), 'MX%d' % (8 + ut)
                else:
                    urows = U3[64:128, :]
                    wi = lambda d, jx, rx: WIN[64:128, d, jx, rx, :]
                    tpi, ukey = (64, 0), 'U3'
                for d in range(2):
                    q8 = d * 4 + P4
                    for b in range(2):
                        ub = urows[:, b * 512:(b + 1) * 512].rearrange("p (c j) -> p c j", j=8)
                        for ri in range(2):
                            pv = PS[:, 2 * ri + b, :].rearrange("p (c j) -> p c j", j=8)
                            for j in range(8):
                                jj = j if d == 0 else 7 - j
                                mm(pv[:, :, jj], wi(d, jj, ri), ub[:, :, j],
                                   j == 0, j == 7, ['WIN', ukey], ['PS%d' % ri], inc=(j == 7),
                                   tile_position=tpi)
                    for ri in range(2):
                        for b in range(2):
                            scan(ST2_[:, d, ri, b * 512:(b + 1) * 512], D0[:], PS[:, 2 * ri + b, :],
                                 ['D0', 'PS%d' % ri], [kst] + hka)
                    stv = ST2_[:, d].rearrange("p r (c j) -> p r c j", j=8)
                    if d == 0:
                        xr_o, xi_o = XR_[:, 0, 1:129], XI_[:, 0, 1:129]
                    else:
                        xr_o, xi_o = XR_[:, 1, 1:129][:, ::-1], XI_[:, 1, 1:129][:, ::-1]
                    p7r, p7i = SM[:, 2, q8:q8 + 1], SM[:, 3, q8:q8 + 1]
                    tmpc, tk = (TMP[:, 2, 0:128], 'T2') if par == 0 else (TMPB[:, 0:128], 'T2b')
                    ts(xr_o, stv[:, 0, :, 7], p7r, None, ALU.mult, None, [kst, 'SM'], [kx] + hka)
                    ts(xi_o, stv[:, 0, :, 7], p7i, None, ALU.mult, None, [kst, 'SM'], [kx])
                    ts(tmpc, stv[:, 1, :, 7], p7i, None, ALU.mult, None, [kst, 'SM'], [tk])
                    tt(xr_o, xr_o, tmpc, ALU.subtract, [kx, tk], [kx])
                    stt(xi_o, stv[:, 1, :, 7], p7r, xi_o, ALU.mult, ALU.add, [kst, 'SM', kx], [kx])
                    hb = (lu * 2) * 8
                    vcopy(XR_[:, d, 0:1], PH[:, hb + q8:hb + q8 + 1], ['PH'], [kx])
                    vcopy(XI_[:, d, 0:1], PH[:, hb + 8 + q8:hb + 8 + q8 + 1], ['PH'], [kx])

            def stage_b(P4):
                par = P4 % 2
                ST2_, CHB_ = ST2P[par], CHBP[par]
                XR_, XI_, XHR_, XHI_, SHR_, SHI_ = XS[par]
                kx, kst, kh, ksh, kc_ = 'X%d' % par, 'ST%d' % par, 'XH%d' % par, 'SH%d' % par, 'CHB%d' % par
                ka, kb = 'TA', 'TB'
                tq = lambda X: X[:, 2 * P4:2 * P4 + 2, :]
                fl = lambda X: X.rearrange("p a b -> p (a b)")
                a1, a2 = TA[:, 2 * par:2 * par + 2, :], TB[:, 2 * par:2 * par + 2, :]
                tt(a1, XR_[:], tq(TR), ALU.mult, [kx, 'TR'], [ka])
                tt(a2, XI_[:], tq(TI), ALU.mult, [kx, 'TI'], [kb])
                tt(XHR_[:], a1, a2, ALU.add, [ka, kb], [kh] + hk)
                tt(a1, XI_[:], tq(TR), ALU.mult, [kx, 'TR'], [ka])
                tt(a2, XR_[:], tq(TI), ALU.mult, [kx, 'TI'], [kb])
                tt(XHI_[:], a1, a2, ALU.subtract, [ka, kb], [kh])
                scan(fl(SHR_[:]), fl(tq(DZ)), fl(XHR_[:]), ['DZ', kh], [ksh] + hk)
                scan(fl(SHI_[:]), fl(tq(DZ)), fl(XHI_[:]), ['DZ', kh], [ksh])
                tt(a1, SHR_[:], tq(TR), ALU.mult, [ksh, 'TR'], [ka])
                tt(a2, SHI_[:], tq(TI), ALU.mult, [ksh, 'TI'], [kb])
                tt(XR_[:], a1, a2, ALU.subtract, [ka, kb], [kx])
                tt(a1, SHR_[:], tq(TI), ALU.mult, [ksh, 'TI'], [ka])
                tt(a2, SHI_[:], tq(TR), ALU.mult, [ksh, 'TR'], [kb])
                tt(XI_[:], a1, a2, ALU.add, [ka, kb], [kx])
                for d in range(2):
                    fo = ((d * 4 + P4) * 2) * 4
                    for ri, X in enumerate((XR_, XI_)):
                        if d == 0:
                            vcopy(FINS[:, fo + ri * 4:fo + ri * 4 + 4], X[:, 0, 32:129:32], [kx], ['FINS'])
                            tt(CHB_[:, 0, ri, :], X[:, 0, 0:128], CMASK[:, 0, :], ALU.mult, [kx, 'CMASK'], [kc_])
                        else:
                            vcopy(FINS[:, fo + ri * 4:fo + ri * 4 + 4][:, ::-1], X[:, 1, 32:129:32], [kx], ['FINS'])
                            tt(CHB_[:, 1, ri, :], X[:, 1, 0:128][:, ::-1], CMASK[:, 1, :], ALU.mult,
                               [kx, 'CMASK'], [kc_])
                for d in range(2):
                    for b in range(2):
                        if P4 < 2:
                            yv = PS[32 * P4:32 * P4 + 32, 4 + b, :].rearrange("p (c j) -> p c j", j=8)
                            wo = lambda jx, rx: WOUT[:, d, jx, rx, P4, :]
                            tpos = (0, 32 * P4)
                        else:
                            yv = PS[64:128, 4 + b, :].rearrange("p (c j) -> p c j", j=8)
                            b0 = 2 if P4 == 2 else 3
                            wo = lambda jx, rx: WOUT[:, d, jx, rx, b0:b0 + 2, :].rearrange("p a b -> p (a b)")
                            tpos = (0, 64)
                        stb = ST2_[:, d, :, b * 512:(b + 1) * 512].rearrange("p r (c j) -> p r c j", j=8)
                        first = True
                        for j in range(8):
                            jj = j if d == 0 else 7 - j
                            for ri in range(2):
                                mm(yv[:, :, j], wo(jj, ri), stb[:, ri, :, jj],
                                   (d == 0 and first and P4 != 3), False, ['WOUT', kst, kc_], ['PS2'], inc=False,
                                   tile_position=tpos)
                                first = False
                                lastmm = (d == 1 and j == 7 and ri == 1)
                                mm(yv[:, :, j], wo(jj + 1, ri), CHB_[:, d, ri, 64 * b:64 * b + 64],
                                   False, lastmm, ['WOUT', kst, kc_], ['PS2'], inc=(j == 7 and ri == 1),
                                   tile_position=tpos)

            def record(fn, P4):
                tr.defer = []
                fn(P4)
                c, tr.defer = tr.defer, None
                return c

            pl = P4LIST[0]
            stage_a(pl[0])
            for i_, P4 in enumerate(pl):
                chains = [record(stage_b, P4)]
                if i_ + 1 < len(pl):
                    chains.append(record(stage_a, pl[i_ + 1]))
                tr.emit_interleaved(chains)
            ld(fin[:, lu * 64:(lu + 1) * 64], FINS[:], ['FINO'], 'fo', r=['FINS'])
            if STOP[0] == 4:
                dump(lambda kc: psb(2), 1, ['PS2'])
                tr.finish()
                return nc
            ysum = TMP[:, 0, :]
            stt(ysum, MX[:, 8 + ut, :], PK8[:, l, 0, ut:ut + 1], psb(2), ALU.mult, ALU.add,
                ['MX%d' % (8 + ut), 'PK8', 'PS2'], ['T0'])
            tt(TMP[:, 1, :], ysum, ysum, ALU.mult, ['T0'], ['T1'])
            ts(TMP[:, 1, :], TMP[:, 1, :], 0.044715, 1.0, ALU.mult, ALU.add, ['T1'], ['T1'])
            tt(TMP[:, 1, :], TMP[:, 1, :], ysum, ALU.mult, ['T1', 'T0'], ['T1'])
            act(TMP[:, 1, :], TMP[:, 1, :], AF.Sigmoid, ['T1'], ['T1'], scale=1.5957691216057308)
            tt(Z[:, ut, :], TMP[:, 1, :], ysum, ALU.mult, ['T1', 'T0'], ['Z%d' % ut])
            if l + 1 < L:
                emit_mods(range(12 * ut, 12 * ut + 12))

        vset(TMPB[:, 0:1], 0.0, ['ST1', 'CHB1', 'X1', 'XH1', 'SH1', 'T2b', 'XT0', 'XT1', 'RSTD', 'SQ'])
        if STOP[0] == 45:
            dump(lambda kc: Z[:, kc, :], 8, ['Z%d' % k for k in range(8)])
            tr.finish()
            return nc
        Zkeys = ['Z%d' % k for k in range(8)]
        for oc in range(8):
            wt, wk = wget()
            big_mm(oc % 2, wt, wk, lambda kc: Z[:, kc, :], Zkeys, 8)
            act(TMP[:, oc % 2, :], psb(oc % 2), AF.Sigmoid, ['PS%d' % (oc % 2), 'PK8'], ['T%d' % (oc % 2)],
                bias=PK8[:, l, 1, oc:oc + 1], scale=1.0)
            tt(SSMO[:, oc, :], TMP[:, oc % 2, :], Z[:, oc, :], ALU.mult, ['T%d' % (oc % 2), 'Z%d' % oc],
               ['SO%d' % oc])
            act(SQ[:], SSMO[:, oc, :], AF.Square, ['SO%d' % oc], ['SQ'])
            for b in range(2):
                mm(PS[:, 4 + b, :], ONES[:], SQ[:, b * 512:(b + 1) * 512], oc == 0, oc == 7, ['SQ', 'ONES'],
                   ['PS2'], inc=(b == 1))
        rstd_from(2, 1024)
        for oc in range(8):
            stt(MX[:, 8 + oc, :], SSMO[:, oc, :], PK8[:, l, 3, oc:oc + 1], RSTD[:], ALU.mult, ALU.mult,
                ['SO%d' % oc, 'PK8', 'RSTD'], ['MX%d' % (8 + oc)])
        if STOP[0] == 46:
            dump(lambda kc: MX[:, kc, :], 16, ['MX%d' % k for k in range(16)])
            tr.finish()
            return nc
        MXkeys = ['MX%d' % k for k in range(16)]
        for oc in range(16):
            wt, wk = wget()
            big_mm(oc % 2, wt, wk, lambda kc: MX[:, kc, :], MXkeys, 16)
            act(MIXED[:, oc, :], psb(oc % 2), AF.Copy, ['PS%d' % (oc % 2)],
                ['MI%d' % oc] + ([SK(i) for i in range(32)] if oc == 0 else []))

        if STOP[0] == 47:
            dump(lambda kc: MIXED[:, kc, :], 16, ['MI%d' % k for k in range(16)])
            tr.finish()
            return nc
        residual(MIXED, 'MI', COEF[:, 1, :])
        if STOP[0] == 5:
            tr.finish()
            return nc
        norm_to_H(COEF[:, 2, :], sh2)
        for fc in range(44):
            bufs = []
            for half in range(2):
                c = fc + 44 * half
                wt, wk = wget()
                pi = (2 * fc + half) % 2
                big_mm(pi, wt, wk, lambda kc: H[:, kc, :], Hkeys, 16)
                ps_ = psb(pi)
                pk = 'PS%d' % pi
                if half == 0:
                    acc, ak = TMP[:, fc % 2 * 2, :], 'T%d' % (fc % 2 * 2)
                else:
                    acc, ak = (TMP[:, 1, :], 'T1') if fc % 2 == 0 else (RSTD[:], 'RSTD')
                act(acc, ps_, AF.Copy, [pk, 'FCW'], [ak], scale=FCW[:, 1, c:c + 1])
                stt(acc[:, 1:T], ps_[:, 0:T - 1], FCD[:, 0, c:c + 1], acc[:, 1:T], ALU.mult, ALU.add,
                    [pk, ak, 'FCD'], [ak])
                stt(acc[:, 0:T - 1], ps_[:, 1:T], FCD[:, 1, c:c + 1], acc[:, 0:T - 1], ALU.mult, ALU.add,
                    [pk, ak, 'FCD'], [ak])
                stt(acc[:, 64:T], ps_[:, 0:T - 64], FCD[:, 2, c:c + 1], acc[:, 64:T], ALU.mult, ALU.add,
                    [pk, ak, 'FCD'], [ak])
                stt(acc[:, 0:T - 64], ps_[:, 64:T], FCD[:, 3, c:c + 1], acc[:, 0:T - 64], ALU.mult, ALU.add,
                    [pk, ak, 'FCD'], [ak])
                p256 = ps_.rearrange("p (r c) -> p r c", c=256)
                a256 = acc.rearrange("p (r c) -> p r c", c=256)
                stt(a256[:, 1:4, 0], p256[:, 0:3, 255], FCD[:, 4, c:c + 1], a256[:, 1:4, 0], ALU.mult, ALU.add,
                    [pk, ak, 'FCD'], [ak])
                stt(a256[:, 0:3, 255], p256[:, 1:4, 0], FCD[:, 5, c:c + 1], a256[:, 0:3, 255], ALU.mult, ALU.add,
                    [pk, ak, 'FCD'], [ak])
                bufs.append((acc, ak))
            (ca, cak), (cg, cgk) = bufs
            act(SQ[:], cg, AF.Silu, [cgk], ['SQ'])
            tt(ACTB[:, fc, :], SQ[:], ca, ALU.mult, ['SQ', cak],
               ['AC%d' % fc] + (['FINS', 'CHB', 'SM', 'XHR', 'XHI', 'SHR', 'SHI'] if fc == 8 else []))
        ACkeys = ['AC%d' % k for k in range(44)]
        for oc in range(16):
            for si, (k0, nk) in enumerate(((0, 16), (16, 16), (32, 12))):
                wt, wk = wget()
                big_mm(oc % 2, wt, wk, lambda kc, k0=k0: ACTB[:, k0 + kc, :], ACkeys, nk, first=(si == 0),
                       last=(si == 2))
            act(H[:, oc, :], psb(oc % 2), AF.Copy, ['PS%d' % (oc % 2)], ['H%d' % oc])
        residual(H, 'H', COEF[:, 3, :])
        if STOP[0] == 6:
            tr.finish()
            return nc

    tr.finish()
    return nc


def _fm(x):
    t, f = x.shape
    return np.ascontiguousarray(x.T.reshape(f // 128, 128, t).transpose(1, 0, 2))


def _vec(v):
    return np.ascontiguousarray(v.reshape(-1, 128).T)


_NC_CACHE = {}


def kernel(x_prompt, x_sample, state_ssm_re, state_ssm_im, c, c_ctx, w_ada, b_ada,
           g_pre1, w_in, conv_w, ssm_a_re, ssm_a_im, ssm_log_dt, ssm_b_re, ssm_b_im,
           ssm_c_re, ssm_c_im, ssm_d, w_glu, b_glu, g_conv_out, g_ssm_out, w_out,
           g_post1, g_pre2, w_up, ffn_conv_w, w_down, g_post2, _ncores=8, _roles=None):
    f32 = np.float32
    A = lambda a: np.asarray(a, dtype=f32)
    x_prompt, x_sample = A(x_prompt), A(x_sample)
    pk16 = np.stack([np.stack([_vec(A(g)[l]) for g in (g_pre1, g_post1, g_pre2, g_post2)], 1) for l in range(L)], 1)
    pk16 = np.ascontiguousarray(pk16.reshape(128, -1))
    bada = np.ascontiguousarray(np.stack([_vec(A(b_ada)[l]) for l in range(L)], 1).reshape(128, -1))
    pk8 = np.stack([np.stack([_vec(A(g)[l]) for g in (ssm_d, b_glu, g_conv_out, g_ssm_out)], 1) for l in range(L)], 1)
    pk8 = np.ascontiguousarray(pk8.reshape(128, -1))
    cwp = np.stack([np.stack([_vec(A(conv_w)[l, k]) for k in range(3)], 1) for l in range(L)], 1).reshape(128, -1)
    fcwp = np.stack([np.stack([_vec(A(ffn_conv_w)[l, k]) for k in range(3)], 1) for l in range(L)], 1).reshape(128, -1)
    are, aim, ldt = A(ssm_a_re), A(ssm_a_im), A(ssm_log_dt)
    bre, bim, cre, cim = A(ssm_b_re), A(ssm_b_im), A(ssm_c_re), A(ssm_c_im)
    parR = np.zeros((L, 8, 4, 2, 16, 5, 2, 2, 64), f32)
    parCC = np.zeros((L, 8, 2, 64, 5, 2, 4, 2, 16), f32)
    for ut in range(8):
        for P4 in range(4):
            for g2 in range(2):
                g = 2 * (4 * ut + P4) + g2
                parR[:, ut, P4, :, :, 0, :, g2, :] = are[:, None, None, :, g, :]
                parR[:, ut, P4, :, :, 1, :, g2, :] = aim[:, None, None, :, g, :]
                parR[:, ut, P4, :, :, 2, :, g2, :] = ldt[:, None, None, :, g, None]
                parR[:, ut, P4, g2, :, 3, :, g2, :] = bre[:, :, g, :, :].transpose(0, 3, 1, 2)
                parR[:, ut, P4, g2, :, 4, :, g2, :] = bim[:, :, g, :, :].transpose(0, 3, 1, 2)
                parCC[:, ut, g2, :, 0, :, P4, :, :] = are[:, :, g, :].transpose(0, 2, 1)[:, :, :, None, None]
                parCC[:, ut, g2, :, 1, :, P4, :, :] = aim[:, :, g, :].transpose(0, 2, 1)[:, :, :, None, None]
                parCC[:, ut, g2, :, 2, :, P4, :, :] = ldt[:, :, g][:, None, :, None, None]
                parCC[:, ut, g2, :, 3, :, P4, g2, :] = cre[:, :, g, :, :].transpose(0, 3, 1, 2)
                parCC[:, ut, g2, :, 4, :, P4, g2, :] = cim[:, :, g, :, :].transpose(0, 3, 1, 2)
    parR = parR.reshape(L * 8, 128, 5 * 256)
    parCC = parCC.reshape(L * 8, 128, 5 * 256)

    def role_arrays(sample, b):
        m = 1.0 if sample else 0.0
        flags = np.tile(np.array([m, 1.0 - m, m, 0.0], f32), (128, 1))
        cf = np.zeros(16, f32)
        if sample:
            cf[:15] = 1.0
        else:
            cf[[3, 7, 11]] = 1.0
        cfix = np.tile(cf, (128, 1))
        cm = np.ones((2, 128), f32)
        rm = np.ones(129, f32)
        rm[0] = 0.0
        if not sample:
            cm[0, [32, 64, 96]] = 0.0
            cm[1, [31, 63, 95]] = 0.0
            rm[[1, 33, 65, 97]] = 0.0
        cmask = np.tile(cm.reshape(1, -1), (128, 1))
        rmask = np.tile(rm.reshape(1, -1), (128, 1))
        ph = np.zeros((L, 8, 2, 2, 4, 2, 64), f32)
        if sample:
            for ri, st in enumerate((A(state_ssm_re), A(state_ssm_im))):
                s = st[b]
                ph[:, :, ri] = s.reshape(L, 2, 8, 4, 2, 64).transpose(0, 2, 1, 3, 4, 5)
        parH = np.ascontiguousarray(ph.transpose(5, 6, 0, 1, 2, 3, 4).reshape(128, -1))
        return flags, cfix, parH, cmask, rmask

    n_used = _ncores
    roles = [(False, 0), (False, 1), (False, 2), (False, 3), (True, 0), (True, 1), (True, 0), (True, 1)][:n_used]
    if _roles is not None:
        roles = _roles
        n_used = len(roles)
    shared = dict(w_ada=A(w_ada), w_in=A(w_in), w_glu=A(w_glu), w_out=A(w_out), w_up=A(w_up), w_down=A(w_down),
                  pk16=pk16, bada=bada, pk8=pk8, cw=np.ascontiguousarray(cwp), fcw=np.ascontiguousarray(fcwp),
                  parR=parR, parCC=parCC)
    in_maps = []
    for (sample, b) in roles:
        if sample:
            xs = x_sample[b]
            cv = A(c)[b]
        else:
            xs = x_prompt[4 * b:4 * b + 4].reshape(1024, D)
            cv = A(c_ctx)
        flags, cfix, parH, cmask, rmask = role_arrays(sample, b)
        m = dict(shared)
        m.update(x0=_fm(xs), cvec=_vec(cv), flags=flags, cfix=cfix, parH=parH, cmask=cmask, rmask=rmask)
        in_maps.append(m)
    if 'nc' not in _NC_CACHE:
        _NC_CACHE['nc'] = build_nc()
    res = run_bass_kernel_spmd(_NC_CACHE['nc'], in_maps, core_ids=list(range(n_used)))
    outs = res.results
    if STOP[0] != 99:
        _NC_CACHE['dbg'] = [o.get("dbg") for o in outs]
        _NC_CACHE['raw'] = outs
    yp = np.zeros((16, 256, D), f32)
    ys = np.zeros((2, 1024, D), f32)
    nre = np.zeros((16, L, 2, 64, 64), f32)
    nim = np.zeros((16, L, 2, 64, 64), f32)
    for ci, (sample, b) in enumerate(roles[:min(6, n_used)]):
        yt = outs[ci]["y"]
        xt = yt.transpose(2, 1, 0).reshape(1024, D)
        if sample:
            ys[b] = xt
        else:
            yp[4 * b:4 * b + 4] = xt.reshape(4, 256, D)
            f = outs[ci]["fin"].reshape(2, 64, L, 8, 2, 4, 2, 4)
            f = f.transpose(7, 6, 2, 4, 3, 5, 0, 1).reshape(4, 2, L, 2, 64, 64)
            nre[4 * b:4 * b + 4] = f[:, 0]
            nim[4 * b:4 * b + 4] = f[:, 1]
    return yp, ys, nre, nim
```

```python
import numpy as np
import concourse.bass as bass
import concourse.mybir as mybir
from concourse.bass_utils import run_bass_kernel_spmd

F32 = mybir.dt.float32
BF16 = mybir.dt.bfloat16
AF = mybir.ActivationFunctionType
ALU = mybir.AluOpType
L = 4
T = 1024
D = 2048
DFF = 5632
EPS = 1e-6
NCH = 257
TWO_PI = 6.283185307179586
MAGIC = 12582912.0


class Tr:
    def __init__(self, nc):
        self.nc = nc
        self.eng = {'pe': nc.tensor, 'act': nc.scalar, 'dve': nc.vector, 'pool': nc.gpsimd, 'sp': nc.sync}
        self.sem = {e: nc.alloc_semaphore('sem_' + e) for e in ('pe', 'act', 'dve', 'pool')}
        self.cnt = {e: 0 for e in self.sem}
        self.waited = {e: {} for e in self.eng}
        self.lastw = {}
        self.readers = {}
        self.dsem = {}

    def _deps(self, reads, writes):
        toks = []
        for r in reads:
            if r in self.lastw:
                toks.append(self.lastw[r])
        for w in writes:
            if w in self.lastw:
                toks.append(self.lastw[w])
            toks += list(self.readers.get(w, {}).values())
        return toks

    def _wait(self, e, toks):
        for (sem, key, val) in toks:
            if self.waited[e].get(key, 0) >= val:
                continue
            self.eng[e].wait_ge(sem, val)
            self.waited[e][key] = val

    def _commit(self, tok, reads, writes):
        for r in reads:
            d = self.readers.setdefault(r, {})
            if d.get(tok[1], (0, 0, 0))[2] < tok[2]:
                d[tok[1]] = tok
        for w in writes:
            self.lastw[w] = tok
            self.readers[w] = {}

    defer = None

    def op(self, e, fn, reads=(), writes=(), inc=True):
        if self.defer is not None:
            self.defer.append(('op', e, fn, list(reads), list(writes), inc))
            return None
        toks = self._deps(reads, writes)
        if e == 'pe':
            toks = [t for t in toks if t[1] != 'pe']
        self._wait(e, toks)
        ins = fn()
        if inc:
            self.cnt[e] += 1
            ins.then_inc(self.sem[e], 1)
            tok = (self.sem[e], e, self.cnt[e])
            self._commit(tok, reads, writes)
        return ins

    def dma(self, q, out, in_, reads, writes, key):
        if self.defer is not None:
            self.defer.append(('dma', q, out, in_, list(reads), list(writes), key))
            return
        toks = self._deps(reads, writes)
        self._wait(q, toks)
        if key not in self.dsem:
            self.dsem[key] = [self.nc.alloc_semaphore('d_' + key), 0]
        ent = self.dsem[key]
        ins = self.eng[q].dma_start(out=out, in_=in_)
        ent[1] += 16
        ins.then_inc(ent[0], 16)
        tok = (ent[0], 'dma_' + key, ent[1])
        self._commit(tok, reads, writes)

    def emit_interleaved(self, chains):
        assert self.defer is None
        n = max(len(c) for c in chains)
        pos = [0] * len(chains)
        for step in range(1, n + 1):
            for ci, c in enumerate(chains):
                tgt = (len(c) * step + n - 1) // n
                while pos[ci] < min(tgt, len(c)):
                    it = c[pos[ci]]
                    pos[ci] += 1
                    if it[0] == 'op':
                        self.op(*it[1:])
                    else:
                        self.dma(*it[1:])

    def finish(self):
        for key, (sem, val) in self.dsem.items():
            self.eng['sp'].wait_ge(sem, val)
        for e in ('pe', 'act', 'dve', 'pool'):
            if self.cnt[e]:
                self.eng['sp'].wait_ge(self.sem[e], self.cnt[e])


STOP = [99]
P4LIST = [[0, 1, 2, 3]]


def build_nc():
    nc = bass.Bass("TRN2", target_bir_lowering=False)
    tr = Tr(nc)

    def din(name, shape, dt=F32):
        return nc.dram_tensor(name, list(shape), dt, kind="ExternalInput").ap()

    x0 = din("x0", [128, 16, T])
    y = nc.dram_tensor("y", [128, 16, T], F32, kind="ExternalOutput").ap()
    fin = nc.dram_tensor("fin", [128, L * 8 * 64], F32, kind="ExternalOutput").ap()
    dbg = nc.dram_tensor("dbg", [128, 16, T], F32, kind="ExternalOutput").ap() if STOP[0] != 99 else None
    cvec = din("cvec", [128, 16])
    flags = din("flags", [128, 4])
    cfix = din("cfix", [128, 16])
    cmask = din("cmask", [128, 2 * 128])
    rmask = din("rmask", [128, 129])
    w_ada = din("w_ada", [L, D, 6 * D])
    w_in = din("w_in", [L, D, 4096])
    w_glu = din("w_glu", [L, 1024, 1024])
    w_out = din("w_out", [L, D, D])
    w_up = din("w_up", [L, D, 2 * DFF])
    w_down = din("w_down", [L, DFF, D])
    pk16 = din("pk16", [128, L * 4 * 16])
    bada = din("bada", [128, L * 96])
    pk8 = din("pk8", [128, L * 4 * 8])
    cw = din("cw", [128, L * 3 * 8])
    fcw = din("fcw", [128, L * 3 * 88])
    parR = din("parR", [L * 8, 128, 5 * 256])
    parCC = din("parCC", [L * 8, 128, 5 * 256])
    parH = din("parH", [128, L * 8 * 16])

    arena = nc.alloc_sbuf_tensor("arena", [128, 212000], mybir.dt.uint8)
    off = [16512]

    def at(name, shape, dt, o=None):
        nbytes = int(np.prod(shape[1:])) * (2 if dt == BF16 else 4)
        if o is None:
            o = off[0]
            off[0] += (nbytes + 31) // 32 * 32
            assert off[0] <= 16512 + 211900, (name, off[0])
        return nc.alloc_sbuf_tensor_at(name, list(shape), dt, offset=o)

    H = at("H", [128, 16, T], BF16)
    MX = at("MX", [128, 16, T], BF16)
    a0 = off[0]
    ACTB = at("ACTB", [128, 44, T], BF16)
    Z = at("Z", [128, 8, T], BF16, a0)
    SSMO = at("SSMO", [128, 8, T], BF16, a0 + 16384)
    MIXED = at("MIXED", [128, 16, T], BF16, a0 + 32768)
    S5S = at("S5S", [128, 32, 256], F32, a0 + 32768)
    ST = at("ST", [128, 2, T], BF16, a0 + 65536)
    WIN = at("WIN", [128, 2, 8, 2, 128], BF16, a0 + 69632)
    WOUT = at("WOUT", [128, 2, 9, 2, 5, 32], BF16, a0 + 77824)
    assert 77824 + 11520 <= 90112
    WT = at("WT", [128, 3, 16, 128], BF16)
    XT = at("XT", [128, 2, T], F32)
    TMP = at("TMP", [128, 3, T], F32)
    RSTD = at("RSTD", [128, T], F32)
    SQ = at("SQ", [128, T], BF16)
    CHB = at("CHB", [128, 2, 2, 128], BF16, a0 + 16384)
    h0_ = 16512
    ST2 = at("ST2", [128, 2, 2, T], BF16, h0_)
    TR = at("TR", [128, 8, 129], F32, h0_ + 8192)
    TI = at("TI", [128, 8, 129], F32, h0_ + 12320)
    DZ = at("DZ", [128, 8, 129], F32, h0_ + 16448)
    TA = at("TA", [128, 8, 129], F32, h0_ + 20576)
    TB = at("TB", [128, 8, 129], F32, h0_ + 24704)
    XR = at("XR", [128, 2, 129], F32, h0_ + 28832)
    XI = at("XI", [128, 2, 129], F32, h0_ + 28832 + 1056)
    XHR = at("XHR", [128, 2, 129], F32, a0 + 16384 + 1024)
    XHI = at("XHI", [128, 2, 129], F32, a0 + 16384 + 1024 + 1056)
    SHR = at("SHR", [128, 2, 129], F32, a0 + 16384 + 1024 + 2112)
    SHI = at("SHI", [128, 2, 129], F32, a0 + 16384 + 1024 + 3168)
    assert 28832 + 2112 <= 32768
    SM = at("SM", [128, 64, 8], F32, a0 + 16384 + 13408)
    FINS = at("FINS", [128, 64], F32, a0 + 16384 + 15456)
    D0 = at("D0", [128, 512], BF16)
    ONES = at("ONES", [128, 128], BF16)
    BONES = at("BONES", [128, 128], BF16)
    MOD2 = at("MOD2", [128, 2, 96], F32)
    BADA = at("BADA", [128, 96], F32)
    PK16 = at("PK16", [128, L, 4, 16], F32)
    PK8 = at("PK8", [128, L, 4, 8], F32)
    CW = at("CW", [128, L, 3, 8], F32)
    FCW = at("FCW", [128, 3, 88], F32)
    FCD = at("FCD", [128, 8, 88], F32)
    CWD = at("CWD", [128, 2, 8], F32)
    CV = at("CV", [128, 16], F32)
    SCB = at("SCB", [128, 16], BF16)
    FLG = at("FLG", [128, 4], F32)
    CFX = at("CFX", [128, 16], F32)
    COEF = at("COEF", [128, 4, 16], F32)
    EPSV = at("EPSV", [128, 4], F32)
    FX = at("FX", [128, 4, 16], F32)
    PH = at("PH", [128, L * 8 * 16], F32)
    U3 = at("U3", [128, T], BF16)
    MASKP = at("MASKP", [128, 8], F32)
    CMASK = at("CMASK", [128, 2, 128], F32)
    RM = at("RM", [128, 129], F32)

    PS = nc.alloc_psum_tensor("PS", [128, 8, 512], F32)

    def psb(i):
        return PS[:, 2 * i:2 * i + 2, :].rearrange("p a b -> p (a b)")

    V, A_, P_, S_ = 'dve', 'act', 'pe', 'sp'

    EH = {'dve': nc.vector, 'pool': nc.gpsimd}
    G = 'dve'

    def tt(out, a, b, op, r, w, e=V):
        tr.op(e, lambda: EH[e].tensor_tensor(out=out, in0=a, in1=b, op=op), r, w)

    def stt(out, a, sc, b, op0, op1, r, w):
        tr.op(V, lambda: nc.vector.scalar_tensor_tensor(out=out, in0=a, scalar=sc, in1=b, op0=op0, op1=op1), r, w)

    def ts(out, a, s1, s2, op0, op1, r, w, e=V):
        if s2 is None:
            tr.op(e, lambda: EH[e].tensor_scalar(out=out, in0=a, scalar1=s1, scalar2=None, op0=op0), r, w)
        else:
            tr.op(e, lambda: EH[e].tensor_scalar(out=out, in0=a, scalar1=s1, scalar2=s2, op0=op0, op1=op1), r, w)

    def act(out, in_, func, r, w, bias=None, scale=None):
        kw = {}
        if bias is not None:
            kw['bias'] = bias
        if scale is not None:
            kw['scale'] = scale
        tr.op(A_, lambda: nc.scalar.activation(out=out, in_=in_, func=func, **kw), r, w)

    def vcopy(out, in_, r, w, e=V):
        tr.op(e, lambda: EH[e].tensor_copy(out=out, in_=in_), r, w)

    def vset(ap, c, w):
        tr.op(V, lambda: nc.vector.memset(ap, c), (), w)

    def recip(out, in_, r, w):
        tr.op(V, lambda: nc.vector.reciprocal(out=out, in_=in_), r, w)

    def mm(out, lhsT, rhs, start, stop, r, w, inc, **kw):
        tr.op(P_, lambda: nc.tensor.matmul(out, lhsT=lhsT, rhs=rhs, start=start, stop=stop, **kw), r, w, inc=inc)

    def ld(out, in_, w, key, q=S_, r=()):
        tr.dma(q, out, in_, list(r), list(w), key)

    def dump(src, n, keys, off_=0):
        for kc in range(n):
            act(TMP[:, 0, :], src(kc), AF.Copy, keys, ['T0'])
            ld(dbg[:, off_ + kc, :], TMP[:, 0, :], ['DBG%d' % (off_ + kc)], 'dbg', r=['T0'])

    ld(CV[:], cvec, ['CV'], 'c0')
    ld(FLG[:], flags, ['FLG'], 'c1')
    ld(CFX[:], cfix, ['CFX'], 'c2')
    ld(PK16[:].rearrange("p a b c -> p (a b c)"), pk16, ['PK16'], 'c3')
    ld(PK8[:].rearrange("p a b c -> p (a b c)"), pk8, ['PK8'], 'c4')
    ld(CW[:].rearrange("p a b c -> p (a b c)"), cw, ['CW'], 'c5')
    ld(PH[:], parH, ['PH'], 'c6')
    ld(CMASK[:].rearrange("p a b -> p (a b)"), cmask, ['CMASK'], 'c10')
    ld(RM[:], rmask, ['RM'], 'c11')
    vset(ONES[:], 1.0, ['ONES'])
    vset(BONES[:], 0.0, ['BONES'])
    vset(BONES[0:64, 0:64], 1.0, ['BONES'])
    vset(BONES[64:128, 64:128], 1.0, ['BONES'])
    vset(D0[:], 1.0, ['D0'])
    vset(D0[:].rearrange("p (c j) -> p c j", j=8)[:, :, 0:1], 0.0, ['D0'])
    vset(MASKP[64:128, :], 1.0, ['MASKP'])
    vset(MASKP[64:96, :], 0.0, ['MASKP'])
    vset(U3[:], 0.0, ['U3'])
    vset(EPSV[:, 0:1], EPS, ['EPSV'])
    vset(EPSV[:, 1:2], 0.0, ['EPSV'])
    act(TMP[:, 0, 0:16], CV[:], AF.Sigmoid, ['CV'], ['T0'])
    tt(SCB[:], TMP[:, 0, 0:16], CV[:], ALU.mult, ['T0', 'CV'], ['SCB'])

    wl = []

    def wtile(w, l, kc0, nkc, oc):
        v = w[l].rearrange("(kc p) n -> p kc n", p=128)
        return (v[:, kc0:kc0 + nkc, oc * 128:(oc + 1) * 128], nkc)

    for jc in range(96):
        wl.append(wtile(w_ada, 0, 0, 16, jc))
    for l in range(L):
        for oc in list(range(24, 32)) + [x for c in range(8) for x in (8 + c, 16 + c, c)]:
            wl.append(wtile(w_in, l, 0, 16, oc))
        if l + 1 < L:
            for jc in range(96):
                wl.append(wtile(w_ada, l + 1, 0, 16, jc))
        for oc in range(8):
            wl.append(wtile(w_glu, l, 0, 8, oc))
        for oc in range(16):
            wl.append(wtile(w_out, l, 0, 16, oc))
        for fc in range(44):
            wl.append(wtile(w_up, l, 0, 16, fc))
            wl.append(wtile(w_up, l, 0, 16, 44 + fc))
        for oc in range(16):
            for (k0, nk) in ((0, 16), (16, 16), (32, 12)):
                wl.append(wtile(w_down, l, k0, nk, oc))
    wstate = {'issued': 0, 'used': 0}

    def wissue():
        i = wstate['issued']
        if i >= len(wl):
            return
        ap, nkc = wl[i]
        s = i % 3
        ld(WT[:, s, 0:nkc, :], ap, ['WT%d' % s], 'w%d' % s, q='pool')
        wstate['issued'] += 1

    def wget():
        i = wstate['used']
        while wstate['issued'] < min(i + 3, len(wl)):
            wissue()
        wstate['used'] += 1
        s = i % 3
        return WT[:, s], 'WT%d' % s

    def wprefetch():
        while wstate['issued'] < min(wstate['used'] + 3, len(wl)):
            wissue()

    xsrc = {'ap': x0}

    def xload(kc, slot):
        ld(XT[:, slot, :], xsrc['ap'][:, kc, :], ['XT%d' % slot], 'x%d' % slot, r=['Y%d' % kc])

    def rstd_from(psi, r):
        act(RSTD[:], psb(psi), AF.Sqrt, ['PS%d' % psi, 'EPSV'], ['RSTD'], bias=EPSV[:, 0:1], scale=1.0 / r)
        recip(RSTD[:], RSTD[:], ['RSTD'], ['RSTD'])

    def norm_to_H(avec, shvec):
        for kc in range(16):
            xload(kc, kc % 2)
            act(SQ[:], XT[:, kc % 2, :], AF.Square, ['XT%d' % (kc % 2)], ['SQ'])
            for b in range(2):
                mm(PS[:, 4 + b, :], ONES[:], SQ[:, b * 512:(b + 1) * 512], kc == 0, kc == 15,
                   ['SQ', 'ONES'], ['PS2'], inc=(b == 1))
        rstd_from(2, D)
        for kc in range(16):
            xload(kc, kc % 2)
            stt(TMP[:, kc % 2, :], XT[:, kc % 2, :], avec[:, kc:kc + 1], RSTD[:], ALU.mult, ALU.mult,
                ['XT%d' % (kc % 2), 'RSTD', 'COEF'], ['T%d' % (kc % 2)])
            act(H[:, kc, :], TMP[:, kc % 2, :], AF.Identity, ['T%d' % (kc % 2), 'MOD'], ['H%d' % kc],
                bias=shvec[:, kc:kc + 1], scale=1.0)

    def residual(SRC, srckey, cvecap):
        for kc in range(16):
            act(SQ[:], SRC[:, kc, :], AF.Square, ['%s%d' % (srckey, kc)], ['SQ'])
            for b in range(2):
                mm(PS[:, 4 + b, :], ONES[:], SQ[:, b * 512:(b + 1) * 512], kc == 0, kc == 15,
                   ['SQ', 'ONES'], ['PS2'], inc=(b == 1))
        rstd_from(2, D)
        for kc in range(16):
            s = kc % 2
            xload(kc, s)
            stt(TMP[:, s, :], SRC[:, kc, :], cvecap[:, kc:kc + 1], RSTD[:], ALU.mult, ALU.mult,
                ['%s%d' % (srckey, kc), 'RSTD', 'COEF'], ['T%d' % s])
            tt(TMP[:, s, :], XT[:, s, :], TMP[:, s, :], ALU.add, ['XT%d' % s, 'T%d' % s], ['T%d' % s])
            ld(y[:, kc, :], TMP[:, s, :], ['Y%d' % kc], 'yo%d' % s, r=['T%d' % s])
        xsrc['ap'] = y

    def big_mm(psi, wt, wkey, rhs, rkeys, nkc, first=True, last=True):
        for b in range(2):
            for kc in range(nkc):
                mm(PS[:, 2 * psi + b, :], wt[:, kc, :], rhs(kc)[:, b * 512:(b + 1) * 512],
                   first and kc == 0, last and kc == nkc - 1,
                   [wkey] + rkeys, ['PS%d' % psi], inc=(b == 1 and kc == nkc - 1))

    Hkeys = ['H%d' % k for k in range(16)]

    for l in range(L):
        def emit_mods(jcs):
            for jc in jcs:
                wt_, wk_ = wget()
                for kc in range(16):
                    mm(PS[:, 6, jc:jc + 1], wt_[:, kc, :], SCB[:, kc:kc + 1], kc == 0, kc == 15,
                       [wk_, 'SCB'], ['PS3'], inc=(kc == 15))

        MOD = MOD2[:, l % 2, :]
        ld(BADA[:], bada[:, l * 96:(l + 1) * 96], ['BADA'], 'c8')
        ld(FCW[:].rearrange("p a b -> p (a b)"), fcw[:, l * 264:(l + 1) * 264], ['FCW'], 'c9')
        if l == 0:
            emit_mods(range(96))
        tt(MOD, PS[:, 6, 0:96], BADA[:], ALU.add, ['PS3', 'BADA'], ['MOD'])
        sh1, sc1, gt1, sh2, sc2, gt2 = [MOD[:, i * 16:(i + 1) * 16] for i in range(6)]
        for i, (g, sc) in enumerate(((0, sc1), (2, sc2))):
            stt(COEF[:, 2 * i, :], sc, 1.0, PK16[:, l, g, :], ALU.add, ALU.mult, ['MOD', 'PK16'], ['COEF'])
        for i, (g, gt) in enumerate(((1, gt1), (3, gt2))):
            tt(COEF[:, 2 * i + 1, :], gt, PK16[:, l, g, :], ALU.mult, ['MOD', 'PK16'], ['COEF'])
        ts(CWD[:, 0, :], CW[:, l, 0, :], -1.0, None, ALU.mult, None, ['CW'], ['CWD'])
        ts(CWD[:, 1, :], CW[:, l, 2, :], -1.0, None, ALU.mult, None, ['CW'], ['CWD'])
        ts(FCD[:, 0, :], FCW[:, 0, :], FLG[:, 1:2], None, ALU.mult, None, ['FCW', 'FLG'], ['FCD'])
        ts(FCD[:, 1, :], FCW[:, 2, :], FLG[:, 1:2], None, ALU.mult, None, ['FCW', 'FLG'], ['FCD'])
        ts(FCD[:, 2, :], FCW[:, 0, :], FLG[:, 2:3], None, ALU.mult, None, ['FCW', 'FLG'], ['FCD'])
        ts(FCD[:, 3, :], FCW[:, 2, :], FLG[:, 2:3], None, ALU.mult, None, ['FCW', 'FLG'], ['FCD'])
        ts(FCD[:, 4, :], FCD[:, 0, :], -1.0, None, ALU.mult, None, ['FCD'], ['FCD'])
        ts(FCD[:, 5, :], FCD[:, 1, :], -1.0, None, ALU.mult, None, ['FCD'], ['FCD'])

        if STOP[0] == 0:
            tr.finish()
            return nc
        norm_to_H(COEF[:, 0, :], sh1)
        if STOP[0] == 1:
            dump(lambda kc: H[:, kc, :], 16, Hkeys)
            tr.finish()
            return nc
        for ut in range(8):
            wt, wk = wget()
            big_mm(ut % 2, wt, wk, lambda kc: H[:, kc, :], Hkeys, 16)
            act(MX[:, 8 + ut, :], psb(ut % 2), AF.Copy, ['PS%d' % (ut % 2)], ['MX%d' % (8 + ut)])
        for c in range(8):
            wt, wk = wget()
            big_mm(0, wt, wk, lambda kc: H[:, kc, :], Hkeys, 16)
            act(TMP[:, 0, :], psb(0), AF.Copy, ['PS0'], ['T0'])
            wt, wk = wget()
            big_mm(1, wt, wk, lambda kc: H[:, kc, :], Hkeys, 16)
            tt(TMP[:, 1, :], psb(1), TMP[:, 0, :], ALU.mult, ['PS1', 'T0'], ['T1'])
            act(TMP[:, 2, :], TMP[:, 1, :], AF.Copy, ['T1', 'CW'], ['T2'], scale=CW[:, l, 1, c:c + 1])
            stt(TMP[:, 2, 1:T], TMP[:, 1, 0:T - 1], CW[:, l, 0, c:c + 1], TMP[:, 2, 1:T], ALU.mult, ALU.add,
                ['T1', 'T2', 'CW'], ['T2'])
            stt(TMP[:, 2, 0:T - 1], TMP[:, 1, 1:T], CW[:, l, 2, c:c + 1], TMP[:, 2, 0:T - 1], ALU.mult, ALU.add,
                ['T1', 'T2', 'CW'], ['T2'])
            v64 = TMP[:, 1, :].rearrange("p (r c) -> p r c", c=64)
            o64 = TMP[:, 2, :].rearrange("p (r c) -> p r c", c=64)
            tt(FX[:, 0, 0:15], v64[:, 0:15, 63], CFX[:, 0:15], ALU.mult, ['T1', 'CFX'], ['FX'])
            stt(o64[:, 1:16, 0], FX[:, 0, 0:15], CWD[:, 0, c:c + 1], o64[:, 1:16, 0], ALU.mult, ALU.add,
                ['FX', 'T2', 'CWD'], ['T2'])
            tt(FX[:, 1, 0:15], v64[:, 1:16, 0], CFX[:, 0:15], ALU.mult, ['T1', 'CFX'], ['FX'])
            stt(o64[:, 0:15, 63], FX[:, 1, 0:15], CWD[:, 1, c:c + 1], o64[:, 0:15, 63], ALU.mult, ALU.add,
                ['FX', 'T2', 'CWD'], ['T2'])
            wt, wk = wget()
            big_mm(0, wt, wk, lambda kc: H[:, kc, :], Hkeys, 16)
            tt(TMP[:, 0, :], psb(0), TMP[:, 2, :], ALU.mult, ['PS0', 'T2'], ['T0'])
            act(SQ[:], TMP[:, 0, :], AF.Square, ['T0'], ['SQ'])
            for b in range(2):
                mm(PS[:, 4 + b, :], BONES[:], SQ[:, b * 512:(b + 1) * 512], True, True, ['SQ', 'BONES'], ['PS2'],
                   inc=(b == 1))
            rstd_from(2, 64)
            stt(MX[:, c, :], TMP[:, 0, :], PK8[:, l, 2, c:c + 1], RSTD[:], ALU.mult, ALU.mult,
                ['T0', 'RSTD', 'PK8'], ['MX%d' % c])

        if STOP[0] == 2:
            dump(lambda kc: MX[:, kc, :], 16, ['MX%d' % k for k in range(16)])
            tr.finish()
            return nc
        SA = lambda i: S5S[:, i, :]
        SK = lambda i: 'S%d' % i

        def cmul(orr, oi, ar, ai, br, bi, t1, t2, e=V):
            tt(SA(t1), SA(ar), SA(br), ALU.mult, [SK(ar), SK(br)], [SK(t1)], e=e)
            tt(SA(t2), SA(ai), SA(bi), ALU.mult, [SK(ai), SK(bi)], [SK(t2)], e=e)
            if orr == ar and oi == ai:
                tt(SA(ar), SA(ar), SA(bi), ALU.mult, [SK(ar), SK(bi)], [SK(ar)], e=e)
                tt(SA(ai), SA(ai), SA(br), ALU.mult, [SK(ai), SK(br)], [SK(ai)], e=e)
                tt(SA(oi), SA(ar), SA(ai), ALU.add, [SK(ar), SK(ai)], [SK(oi)], e=e)
                tt(SA(orr), SA(t1), SA(t2), ALU.subtract, [SK(t1), SK(t2)], [SK(orr)], e=e)
            else:
                tt(SA(t1), SA(t1), SA(t2), ALU.subtract, [SK(t1), SK(t2)], [SK(t1)], e=e)
                tt(SA(t2), SA(ar), SA(bi), ALU.mult, [SK(ar), SK(bi)], [SK(t2)], e=e)
                tt(SA(oi), SA(ai), SA(br), ALU.mult, [SK(ai), SK(br)], [SK(oi)], e=e)
                tt(SA(oi), SA(oi), SA(t2), ALU.add, [SK(oi), SK(t2)], [SK(oi)], e=e)
                vcopy(SA(orr), SA(t1), [SK(t1)], [SK(orr)], e=e)

        def sincos(li, so, co, t1, e=V):
            for (dst, shift) in ((so, 0.0), (co, np.pi / 2)):
                ts(SA(t1), SA(li), shift, 1.0 / TWO_PI, ALU.add, ALU.mult, [SK(li)], [SK(t1)], e=e)
                ts(SA(t1), SA(t1), MAGIC, None, ALU.add, None, [SK(t1)], [SK(t1)], e=e)
                ts(SA(t1), SA(t1), -MAGIC, None, ALU.add, None, [SK(t1)], [SK(t1)], e=e)
                ts(SA(t1), SA(t1), -TWO_PI, None, ALU.mult, None, [SK(t1)], [SK(t1)], e=e)
                tt(SA(t1), SA(t1), SA(li), ALU.add, [SK(t1), SK(li)], [SK(t1)], e=e)
                ts(SA(t1), SA(t1), shift, None, ALU.add, None, [SK(t1)], [SK(t1)], e=e)
                ts(SA(t1), SA(t1), 3.1415925, -3.1415925, ALU.min, ALU.max, [SK(t1)], [SK(t1)], e=e)
                act(SA(dst), SA(t1), AF.Sin, [SK(t1)], [SK(dst)])

        def abar_from(base, o_r, o_i, t, e=V):
            act(SA(t), SA(base + 2), AF.Exp, [SK(base + 2)], [SK(t)])
            tt(SA(t + 1), SA(base), SA(t), ALU.mult, [SK(base), SK(t)], [SK(t + 1)], e=e)
            tt(SA(t + 2), SA(base + 1), SA(t), ALU.mult, [SK(base + 1), SK(t)], [SK(t + 2)], e=e)
            act(SA(t + 1), SA(t + 1), AF.Exp, [SK(t + 1)], [SK(t + 1)])
            sincos(t + 2, o_i, o_r, t + 3, e=e)
            tt(SA(o_r), SA(o_r), SA(t + 1), ALU.mult, [SK(o_r), SK(t + 1)], [SK(o_r)], e=e)
            tt(SA(o_i), SA(o_i), SA(t + 1), ALU.mult, [SK(o_i), SK(t + 1)], [SK(o_i)], e=e)

        ack = ['AC%d' % k for k in range(8, 16)]
        vset(CHB[:].rearrange("p a b c -> p (a b c)"), 0.0, ['CHB'] + ack)
        vset(SM[:].rearrange("p a b -> p (a b)"), 0.0, ['SM'] + ack)
        vset(FINS[:], 0.0, ['FINS'] + ack)
        for d_ in range(2):
            vset(WOUT[:, d_, :, :, 3, :], 0.0, ['WOUT'])
        for ut in range(8):
            lu = l * 8 + ut
            tr.defer = []
            ld(S5S[:, 0:5, :].rearrange("p a b -> p (a b)"), parR[lu], [SK(i) for i in range(5)], 'pr')
            abar_from(0, 5, 6, 7, e=G)
            tt(SA(7), SA(0), SA(0), ALU.mult, [SK(0)], [SK(7)], e=G)
            tt(SA(8), SA(1), SA(1), ALU.mult, [SK(1)], [SK(8)], e=G)
            tt(SA(7), SA(7), SA(8), ALU.add, [SK(7), SK(8)], [SK(7)], e=G)
            recip(SA(7), SA(7), [SK(7)], [SK(7)])
            ts(SA(8), SA(5), -1.0, None, ALU.add, None, [SK(5)], [SK(8)], e=G)
            tt(SA(9), SA(8), SA(0), ALU.mult, [SK(8), SK(0)], [SK(9)], e=G)
            tt(SA(10), SA(6), SA(1), ALU.mult, [SK(6), SK(1)], [SK(10)], e=G)
            tt(SA(9), SA(9), SA(10), ALU.add, [SK(9), SK(10)], [SK(9)], e=G)
            tt(SA(9), SA(9), SA(7), ALU.mult, [SK(9), SK(7)], [SK(9)], e=G)
            tt(SA(10), SA(6), SA(0), ALU.mult, [SK(6), SK(0)], [SK(10)], e=G)
            tt(SA(11), SA(8), SA(1), ALU.mult, [SK(8), SK(1)], [SK(11)], e=G)
            tt(SA(10), SA(10), SA(11), ALU.subtract, [SK(10), SK(11)], [SK(10)], e=G)
            tt(SA(10), SA(10), SA(7), ALU.mult, [SK(10), SK(7)], [SK(10)], e=G)
            cmul(11, 12, 9, 10, 3, 4, 13, 14, e=G)
            tt(SA(13), SA(5), SA(5), ALU.mult, [SK(5)], [SK(13)], e=G)
            tt(SA(14), SA(6), SA(6), ALU.mult, [SK(6)], [SK(14)], e=G)
            tt(SA(13), SA(13), SA(14), ALU.add, [SK(13), SK(14)], [SK(13)], e=G)
            recip(SA(13), SA(13), [SK(13)], [SK(13)])
            tt(SA(7), SA(5), SA(13), ALU.mult, [SK(5), SK(13)], [SK(7)], e=G)
            ts(SA(8), SA(6), -1.0, None, ALU.mult, None, [SK(6)], [SK(8)], e=G)
            tt(SA(8), SA(8), SA(13), ALU.mult, [SK(8), SK(13)], [SK(8)], e=G)
            for j in range(8):
                for d in range(2):
                    act(WIN[:, d, j, 0, :], S5S[:, 11, d * 128:(d + 1) * 128], AF.Copy, [SK(11)], ['WIN'])
                    act(WIN[:, d, j, 1, :], S5S[:, 12, d * 128:(d + 1) * 128], AF.Copy, [SK(12)], ['WIN'])
                if j < 7:
                    cmul(11, 12, 11, 12, 7, 8, 13, 14, e=G)
            chain_r, tr.defer = tr.defer, []
            ld(S5S[:, 16:21, :].rearrange("p a b -> p (a b)"), parCC[lu], [SK(i) for i in range(16, 21)], 'pc')
            abar_from(16, 21, 22, 23)
            cc = lambda i: S5S[:, i, :].rearrange("p (d q h) -> p d q h", d=2, q=4)
            sm = lambda i: SM[:, i, :].rearrange("p (d q) -> p d q", d=2)
            for j in range(9):
                act(WOUT[:, :, j, 0, 0:3, :], cc(19)[:, :, 0:3, :], AF.Copy, [SK(19)], ['WOUT'])
                act(WOUT[:, :, j, 0, 4, :], cc(19)[:, :, 3, :], AF.Copy, [SK(19)], ['WOUT'])
                act(WOUT[:, :, j, 1, 0:3, :], cc(20)[:, :, 0:3, :], AF.Copy, [SK(20)], ['WOUT'], scale=-1.0)
                act(WOUT[:, :, j, 1, 4, :], cc(20)[:, :, 3, :], AF.Copy, [SK(20)], ['WOUT'], scale=-1.0)
                if j < 8:
                    cmul(19, 20, 19, 20, 21, 22, 25, 26)
                if j == 6:
                    pass
            vcopy(sm(0), cc(21)[:, :, :, 0], [SK(21)], ['SM'])
            vcopy(sm(1), cc(22)[:, :, :, 0], [SK(22)], ['SM'])

            def smul(orr, oi, ar, ai, br, bi):
                tt(SM[:, 60, :], SM[:, ar, :], SM[:, br, :], ALU.mult, ['SM'], ['SM'])
                tt(SM[:, 61, :], SM[:, ai, :], SM[:, bi, :], ALU.mult, ['SM'], ['SM'])
                tt(SM[:, 60, :], SM[:, 60, :], SM[:, 61, :], ALU.subtract, ['SM'], ['SM'])
                tt(SM[:, 61, :], SM[:, ar, :], SM[:, bi, :], ALU.mult, ['SM'], ['SM'])
                tt(SM[:, 62, :], SM[:, ai, :], SM[:, br, :], ALU.mult, ['SM'], ['SM'])
                tt(SM[:, oi, :], SM[:, 61, :], SM[:, 62, :], ALU.add, ['SM'], ['SM'])
                vcopy(SM[:, orr, :], SM[:, 60, :], ['SM'], ['SM'])

            vcopy(SM[:, 2, :], SM[:, 0, :], ['SM'], ['SM'])
            vcopy(SM[:, 3, :], SM[:, 1, :], ['SM'], ['SM'])
            for _ in range(6):
                smul(2, 3, 2, 3, 0, 1)
            smul(4, 5, 2, 3, 0, 1)
            tt(SM[:, 60, :], SM[:, 4, :], SM[:, 4, :], ALU.mult, ['SM'], ['SM'])
            tt(SM[:, 61, :], SM[:, 5, :], SM[:, 5, :], ALU.mult, ['SM'], ['SM'])
            tt(SM[:, 60, :], SM[:, 60, :], SM[:, 61, :], ALU.add, ['SM'], ['SM'])
            act(SM[:, 30, :], SM[:, 60, :], AF.Sqrt, ['SM'], ['SM'])
            recip(SM[:, 31, :], SM[:, 30, :], ['SM'], ['SM'])
            tt(SM[:, 6, :], SM[:, 4, :], SM[:, 31, :], ALU.mult, ['SM'], ['SM'])
            tt(SM[:, 7, :], SM[:, 5, :], SM[:, 31, :], ALU.mult, ['SM'], ['SM'])
            for k in range(1, 8):
                smul(6 + 2 * k, 7 + 2 * k, 4 + 2 * k, 5 + 2 * k, 4 + 2 * k, 5 + 2 * k)
            v4 = lambda X: X[:].rearrange("p (q d) i -> p q d i", d=2)
            cfv = lambda idx, n: SM[:, idx, :].rearrange("p (d q) -> p q d", d=2).unsqueeze(3).to_broadcast(
                [128, 4, 2, n])
            hk = Hkeys if ut == 0 else []
            vset(TR[:, :, 0:1], 1.0, ['TR'] + hk)
            vset(TI[:, :, 0:1], 0.0, ['TI'] + hk)
            for k in range(8):
                n = (1 << k) if k < 7 else 1
                s0, s1, d0_, d1_ = (0, n, n, 2 * n) if k < 7 else (0, 1, 128, 129)
                ur, ui = cfv(6 + 2 * k, n), cfv(7 + 2 * k, n)
                ta, tb = v4(TA)[:, :, :, 0:n], v4(TB)[:, :, :, 0:n]
                tt(ta, v4(TR)[:, :, :, s0:s1], ur, ALU.mult, ['TR', 'SM'], ['TA'])
                tt(tb, v4(TI)[:, :, :, s0:s1], ui, ALU.mult, ['TI', 'SM'], ['TB'])
                tt(v4(TR)[:, :, :, d0_:d1_], ta, tb, ALU.subtract, ['TA', 'TB'], ['TR'])
                tt(ta, v4(TR)[:, :, :, s0:s1], ui, ALU.mult, ['TR', 'SM'], ['TA'])
                tt(tb, v4(TI)[:, :, :, s0:s1], ur, ALU.mult, ['TI', 'SM'], ['TB'])
                tt(v4(TI)[:, :, :, d0_:d1_], ta, tb, ALU.add, ['TA', 'TB'], ['TI'])
            tt(v4(DZ), cfv(30, 129), RM[:].rearrange("p (a b i) -> p a b i", a=1, b=1).to_broadcast([128, 4, 2, 129]),
               ALU.mult, ['SM', 'RM'], ['DZ'] + hk)
            chain_c, tr.defer = tr.defer, None
            tr.emit_interleaved([chain_r, chain_c])
            if STOP[0] == 3:
                tr.finish()
                return nc

            ts(U3[64:128, :], MX[64:128, 8 + ut, :], MASKP[64:128, 0:1], None, ALU.mult, None,
               ['MX%d' % (8 + ut), 'MASKP'], ['U3'])
            for P4 in P4LIST[0]:
                if P4 < 3:
                    urows = MX[32 * P4:32 * P4 + 32, 8 + ut, :]
                    wi = lambda d, jx, rx: WIN[32 * P4:32 * P4 + 32, d, jx, rx, :]
                    tpi, ukey = (32 * P4, 0), 'MX%d' % (8 + ut)
                else:
                    urows = U3[64:128, :]
                    wi = lambda d, jx, rx: WIN[64:128, d, jx, rx, :]
                    tpi, ukey = (64, 0), 'U3'
                for d in range(2):
                    q8 = d * 4 + P4
                    for b in range(2):
                        ub = urows[:, b * 512:(b + 1) * 512].rearrange("p (c j) -> p c j", j=8)
                        for ri in range(2):
                            pv = PS[:, 2 * ri + b, :].rearrange("p (c j) -> p c j", j=8)
                            for j in range(8):
                                jj = j if d == 0 else 7 - j
                                mm(pv[:, :, jj], wi(d, jj, ri), ub[:, :, j],
                                   j == 0, j == 7, ['WIN', ukey], ['PS%d' % ri], inc=(j == 7),
                                   tile_position=tpi)
                    for ri in range(2):
                        for b in range(2):
                            tr.op(V, lambda: nc.vector.tensor_tensor_scan(
                                out=ST2[:, d, ri, b * 512:(b + 1) * 512], data0=D0[:], data1=PS[:, 2 * ri + b, :],
                                initial=0.0, op0=ALU.mult, op1=ALU.add), ['D0', 'PS%d' % ri], ['ST'] + hk)
                    stv = ST2[:, d].rearrange("p r (c j) -> p r c j", j=8)
                    if d == 0:
                        xr_o, xi_o = XR[:, 0, 1:129], XI[:, 0, 1:129]
                    else:
                        xr_o, xi_o = XR[:, 1, 1:129][:, ::-1], XI[:, 1, 1:129][:, ::-1]
                    p7r, p7i = SM[:, 2, q8:q8 + 1], SM[:, 3, q8:q8 + 1]
                    ts(xr_o, stv[:, 0, :, 7], p7r, None, ALU.mult, None, ['ST', 'SM'], ['XR'] + hk)
                    ts(xi_o, stv[:, 0, :, 7], p7i, None, ALU.mult, None, ['ST', 'SM'], ['XI'] + hk)
                    ts(TMP[:, 2, 0:128], stv[:, 1, :, 7], p7i, None, ALU.mult, None, ['ST', 'SM'], ['T2'])
                    tt(xr_o, xr_o, TMP[:, 2, 0:128], ALU.subtract, ['XR', 'T2'], ['XR'])
                    stt(xi_o, stv[:, 1, :, 7], p7r, xi_o, ALU.mult, ALU.add, ['ST', 'SM', 'XI'], ['XI'])
                    hb = (lu * 2) * 8
                    vcopy(XR[:, d, 0:1], PH[:, hb + q8:hb + q8 + 1], ['PH'], ['XR'])
                    vcopy(XI[:, d, 0:1], PH[:, hb + 8 + q8:hb + 8 + q8 + 1], ['PH'], ['XI'])
                tq = lambda X: X[:, 2 * P4:2 * P4 + 2, :]
                fl = lambda X: X.rearrange("p a b -> p (a b)")
                a1, a2 = TA[:, 0:2, :], TB[:, 0:2, :]
                tt(a1, XR[:], tq(TR), ALU.mult, ['XR', 'TR'], ['TA'])
                tt(a2, XI[:], tq(TI), ALU.mult, ['XI', 'TI'], ['TB'])
                tt(XHR[:], a1, a2, ALU.add, ['TA', 'TB'], ['XHR'] + hk)
                tt(a1, XI[:], tq(TR), ALU.mult, ['XI', 'TR'], ['TA'])
                tt(a2, XR[:], tq(TI), ALU.mult, ['XR', 'TI'], ['TB'])
                tt(XHI[:], a1, a2, ALU.subtract, ['TA', 'TB'], ['XHI'] + hk)
                tr.op(V, lambda: nc.vector.tensor_tensor_scan(out=fl(SHR[:]), data0=fl(tq(DZ)), data1=fl(XHR[:]),
                                                              initial=0.0, op0=ALU.mult, op1=ALU.add),
                      ['DZ', 'XHR'], ['SHR'] + hk)
                tr.op(V, lambda: nc.vector.tensor_tensor_scan(out=fl(SHI[:]), data0=fl(tq(DZ)), data1=fl(XHI[:]),
                                                              initial=0.0, op0=ALU.mult, op1=ALU.add),
                      ['DZ', 'XHI'], ['SHI'] + hk)
                tt(a1, SHR[:], tq(TR), ALU.mult, ['SHR', 'TR'], ['TA'])
                tt(a2, SHI[:], tq(TI), ALU.mult, ['SHI', 'TI'], ['TB'])
                tt(XR[:], a1, a2, ALU.subtract, ['TA', 'TB'], ['XR'])
                tt(a1, SHR[:], tq(TI), ALU.mult, ['SHR', 'TI'], ['TA'])
                tt(a2, SHI[:], tq(TR), ALU.mult, ['SHI', 'TR'], ['TB'])
                tt(XI[:], a1, a2, ALU.add, ['TA', 'TB'], ['XI'])
                for d in range(2):
                    fo = ((d * 4 + P4) * 2) * 4
                    for ri, X in enumerate((XR, XI)):
                        if d == 0:
                            vcopy(FINS[:, fo + ri * 4:fo + ri * 4 + 4], X[:, 0, 32:129:32], ['XR', 'XI'], ['FINS'])
                            tt(CHB[:, 0, ri, :], X[:, 0, 0:128], CMASK[:, 0, :], ALU.mult, ['XR', 'XI', 'CMASK'], ['CHB'])
                        else:
                            vcopy(FINS[:, fo + ri * 4:fo + ri * 4 + 4][:, ::-1], X[:, 1, 32:129:32], ['XR', 'XI'],
                                  ['FINS'])
                            tt(CHB[:, 1, ri, :], X[:, 1, 0:128][:, ::-1], CMASK[:, 1, :], ALU.mult,
                               ['XR', 'XI', 'CMASK'], ['CHB'])
                for d in range(2):
                    for b in range(2):
                        if P4 < 2:
                            yv = PS[32 * P4:32 * P4 + 32, 4 + b, :].rearrange("p (c j) -> p c j", j=8)
                            wo = lambda jx, rx: WOUT[:, d, jx, rx, P4, :]
                            tpos = (0, 32 * P4)
                        else:
                            yv = PS[64:128, 4 + b, :].rearrange("p (c j) -> p c j", j=8)
                            b0 = 2 if P4 == 2 else 3
                            wo = lambda jx, rx: WOUT[:, d, jx, rx, b0:b0 + 2, :].rearrange("p a b -> p (a b)")
                            tpos = (0, 64)
                        stb = ST2[:, d, :, b * 512:(b + 1) * 512].rearrange("p r (c j) -> p r c j", j=8)
                        first = True
                        for j in range(8):
                            jj = j if d == 0 else 7 - j
                            for ri in range(2):
                                mm(yv[:, :, j], wo(jj, ri), stb[:, ri, :, jj],
                                   (d == 0 and first and P4 != 3), False, ['WOUT', 'ST', 'CHB'], ['PS2'], inc=False,
                                   tile_position=tpos)
                                first = False
                                lastmm = (d == 1 and j == 7 and ri == 1)
                                mm(yv[:, :, j], wo(jj + 1, ri), CHB[:, d, ri, 64 * b:64 * b + 64],
                                   False, lastmm, ['WOUT', 'ST', 'CHB'], ['PS2'], inc=(j == 7 and ri == 1),
                                   tile_position=tpos)
            ld(fin[:, lu * 64:(lu + 1) * 64], FINS[:], ['FINO'], 'fo', r=['FINS'])
            if STOP[0] == 4:
                dump(lambda kc: psb(2), 1, ['PS2'])
                tr.finish()
                return nc
            ysum = TMP[:, 0, :]
            stt(ysum, MX[:, 8 + ut, :], PK8[:, l, 0, ut:ut + 1], psb(2), ALU.mult, ALU.add,
                ['MX%d' % (8 + ut), 'PK8', 'PS2'], ['T0'])
            tt(TMP[:, 1, :], ysum, ysum, ALU.mult, ['T0'], ['T1'])
            ts(TMP[:, 1, :], TMP[:, 1, :], 0.044715, 1.0, ALU.mult, ALU.add, ['T1'], ['T1'])
            tt(TMP[:, 1, :], TMP[:, 1, :], ysum, ALU.mult, ['T1', 'T0'], ['T1'])
            act(TMP[:, 1, :], TMP[:, 1, :], AF.Sigmoid, ['T1'], ['T1'], scale=1.5957691216057308)
            tt(Z[:, ut, :], TMP[:, 1, :], ysum, ALU.mult, ['T1', 'T0'], ['Z%d' % ut])
            if l + 1 < L:
                emit_mods(range(12 * ut, 12 * ut + 12))

        if STOP[0] == 45:
            dump(lambda kc: Z[:, kc, :], 8, ['Z%d' % k for k in range(8)])
            tr.finish()
            return nc
        Zkeys = ['Z%d' % k for k in range(8)]
        for oc in range(8):
            wt, wk = wget()
            big_mm(oc % 2, wt, wk, lambda kc: Z[:, kc, :], Zkeys, 8)
            act(TMP[:, oc % 2, :], psb(oc % 2), AF.Sigmoid, ['PS%d' % (oc % 2), 'PK8'], ['T%d' % (oc % 2)],
                bias=PK8[:, l, 1, oc:oc + 1], scale=1.0)
            tt(SSMO[:, oc, :], TMP[:, oc % 2, :], Z[:, oc, :], ALU.mult, ['T%d' % (oc % 2), 'Z%d' % oc],
               ['SO%d' % oc])
            act(SQ[:], SSMO[:, oc, :], AF.Square, ['SO%d' % oc], ['SQ'])
            for b in range(2):
                mm(PS[:, 4 + b, :], ONES[:], SQ[:, b * 512:(b + 1) * 512], oc == 0, oc == 7, ['SQ', 'ONES'],
                   ['PS2'], inc=(b == 1))
        rstd_from(2, 1024)
        for oc in range(8):
            stt(MX[:, 8 + oc, :], SSMO[:, oc, :], PK8[:, l, 3, oc:oc + 1], RSTD[:], ALU.mult, ALU.mult,
                ['SO%d' % oc, 'PK8', 'RSTD'], ['MX%d' % (8 + oc)])
        if STOP[0] == 46:
            dump(lambda kc: MX[:, kc, :], 16, ['MX%d' % k for k in range(16)])
            tr.finish()
            return nc
        MXkeys = ['MX%d' % k for k in range(16)]
        for oc in range(16):
            wt, wk = wget()
            big_mm(oc % 2, wt, wk, lambda kc: MX[:, kc, :], MXkeys, 16)
            act(MIXED[:, oc, :], psb(oc % 2), AF.Copy, ['PS%d' % (oc % 2)],
                ['MI%d' % oc] + ([SK(i) for i in range(32)] if oc == 0 else []))

        if STOP[0] == 47:
            dump(lambda kc: MIXED[:, kc, :], 16, ['MI%d' % k for k in range(16)])
            tr.finish()
            return nc
        residual(MIXED, 'MI', COEF[:, 1, :])
        if STOP[0] == 5:
            tr.finish()
            return nc
        norm_to_H(COEF[:, 2, :], sh2)
        for fc in range(44):
            bufs = []
            for half in range(2):
                c = fc + 44 * half
                wt, wk = wget()
                pi = (2 * fc + half) % 2
                big_mm(pi, wt, wk, lambda kc: H[:, kc, :], Hkeys, 16)
                ps_ = psb(pi)
                pk = 'PS%d' % pi
                if half == 0:
                    acc, ak = TMP[:, fc % 2 * 2, :], 'T%d' % (fc % 2 * 2)
                else:
                    acc, ak = (TMP[:, 1, :], 'T1') if fc % 2 == 0 else (RSTD[:], 'RSTD')
                act(acc, ps_, AF.Copy, [pk, 'FCW'], [ak], scale=FCW[:, 1, c:c + 1])
                stt(acc[:, 1:T], ps_[:, 0:T - 1], FCD[:, 0, c:c + 1], acc[:, 1:T], ALU.mult, ALU.add,
                    [pk, ak, 'FCD'], [ak])
                stt(acc[:, 0:T - 1], ps_[:, 1:T], FCD[:, 1, c:c + 1], acc[:, 0:T - 1], ALU.mult, ALU.add,
                    [pk, ak, 'FCD'], [ak])
                stt(acc[:, 64:T], ps_[:, 0:T - 64], FCD[:, 2, c:c + 1], acc[:, 64:T], ALU.mult, ALU.add,
                    [pk, ak, 'FCD'], [ak])
                stt(acc[:, 0:T - 64], ps_[:, 64:T], FCD[:, 3, c:c + 1], acc[:, 0:T - 64], ALU.mult, ALU.add,
                    [pk, ak, 'FCD'], [ak])
                p256 = ps_.rearrange("p (r c) -> p r c", c=256)
                a256 = acc.rearrange("p (r c) -> p r c", c=256)
                stt(a256[:, 1:4, 0], p256[:, 0:3, 255], FCD[:, 4, c:c + 1], a256[:, 1:4, 0], ALU.mult, ALU.add,
                    [pk, ak, 'FCD'], [ak])
                stt(a256[:, 0:3, 255], p256[:, 1:4, 0], FCD[:, 5, c:c + 1], a256[:, 0:3, 255], ALU.mult, ALU.add,
                    [pk, ak, 'FCD'], [ak])
                bufs.append((acc, ak))
            (ca, cak), (cg, cgk) = bufs
            act(SQ[:], cg, AF.Silu, [cgk], ['SQ'])
            tt(ACTB[:, fc, :], SQ[:], ca, ALU.mult, ['SQ', cak],
               ['AC%d' % fc] + (['FINS', 'CHB', 'SM', 'XHR', 'XHI', 'SHR', 'SHI'] if fc == 8 else []))
        ACkeys = ['AC%d' % k for k in range(44)]
        for oc in range(16):
            for si, (k0, nk) in enumerate(((0, 16), (16, 16), (32, 12))):
                wt, wk = wget()
                big_mm(oc % 2, wt, wk, lambda kc, k0=k0: ACTB[:, k0 + kc, :], ACkeys, nk, first=(si == 0),
                       last=(si == 2))
            act(H[:, oc, :], psb(oc % 2), AF.Copy, ['PS%d' % (oc % 2)], ['H%d' % oc])
        residual(H, 'H', COEF[:, 3, :])
        if STOP[0] == 6:
            tr.finish()
            return nc

    tr.finish()
    return nc


def _fm(x):
    t, f = x.shape
    return np.ascontiguousarray(x.T.reshape(f // 128, 128, t).transpose(1, 0, 2))


def _vec(v):
    return np.ascontiguousarray(v.reshape(-1, 128).T)


_NC_CACHE = {}


def kernel(x_prompt, x_sample, state_ssm_re, state_ssm_im, c, c_ctx, w_ada, b_ada,
           g_pre1, w_in, conv_w, ssm_a_re, ssm_a_im, ssm_log_dt, ssm_b_re, ssm_b_im,
           ssm_c_re, ssm_c_im, ssm_d, w_glu, b_glu, g_conv_out, g_ssm_out, w_out,
           g_post1, g_pre2, w_up, ffn_conv_w, w_down, g_post2, _ncores=8, _roles=None):
    f32 = np.float32
    A = lambda a: np.asarray(a, dtype=f32)
    x_prompt, x_sample = A(x_prompt), A(x_sample)
    pk16 = np.stack([np.stack([_vec(A(g)[l]) for g in (g_pre1, g_post1, g_pre2, g_post2)], 1) for l in range(L)], 1)
    pk16 = np.ascontiguousarray(pk16.reshape(128, -1))
    bada = np.ascontiguousarray(np.stack([_vec(A(b_ada)[l]) for l in range(L)], 1).reshape(128, -1))
    pk8 = np.stack([np.stack([_vec(A(g)[l]) for g in (ssm_d, b_glu, g_conv_out, g_ssm_out)], 1) for l in range(L)], 1)
    pk8 = np.ascontiguousarray(pk8.reshape(128, -1))
    cwp = np.stack([np.stack([_vec(A(conv_w)[l, k]) for k in range(3)], 1) for l in range(L)], 1).reshape(128, -1)
    fcwp = np.stack([np.stack([_vec(A(ffn_conv_w)[l, k]) for k in range(3)], 1) for l in range(L)], 1).reshape(128, -1)
    are, aim, ldt = A(ssm_a_re), A(ssm_a_im), A(ssm_log_dt)
    bre, bim, cre, cim = A(ssm_b_re), A(ssm_b_im), A(ssm_c_re), A(ssm_c_im)
    parR = np.zeros((L, 8, 4, 2, 16, 5, 2, 2, 64), f32)
    parCC = np.zeros((L, 8, 2, 64, 5, 2, 4, 2, 16), f32)
    for ut in range(8):
        for P4 in range(4):
            for g2 in range(2):
                g = 2 * (4 * ut + P4) + g2
                parR[:, ut, P4, :, :, 0, :, g2, :] = are[:, None, None, :, g, :]
                parR[:, ut, P4, :, :, 1, :, g2, :] = aim[:, None, None, :, g, :]
                parR[:, ut, P4, :, :, 2, :, g2, :] = ldt[:, None, None, :, g, None]
                parR[:, ut, P4, g2, :, 3, :, g2, :] = bre[:, :, g, :, :].transpose(0, 3, 1, 2)
                parR[:, ut, P4, g2, :, 4, :, g2, :] = bim[:, :, g, :, :].transpose(0, 3, 1, 2)
                parCC[:, ut, g2, :, 0, :, P4, :, :] = are[:, :, g, :].transpose(0, 2, 1)[:, :, :, None, None]
                parCC[:, ut, g2, :, 1, :, P4, :, :] = aim[:, :, g, :].transpose(0, 2, 1)[:, :, :, None, None]
                parCC[:, ut, g2, :, 2, :, P4, :, :] = ldt[:, :, g][:, None, :, None, None]
                parCC[:, ut, g2, :, 3, :, P4, g2, :] = cre[:, :, g, :, :].transpose(0, 3, 1, 2)
                parCC[:, ut, g2, :, 4, :, P4, g2, :] = cim[:, :, g, :, :].transpose(0, 3, 1, 2)
    parR = parR.reshape(L * 8, 128, 5 * 256)
    parCC = parCC.reshape(L * 8, 128, 5 * 256)

    def role_arrays(sample, b):
        m = 1.0 if sample else 0.0
        flags = np.tile(np.array([m, 1.0 - m, m, 0.0], f32), (128, 1))
        cf = np.zeros(16, f32)
        if sample:
            cf[:15] = 1.0
        else:
            cf[[3, 7, 11]] = 1.0
        cfix = np.tile(cf, (128, 1))
        cm = np.ones((2, 128), f32)
        rm = np.ones(129, f32)
        rm[0] = 0.0
        if not sample:
            cm[0, [32, 64, 96]] = 0.0
            cm[1, [31, 63, 95]] = 0.0
            rm[[1, 33, 65, 97]] = 0.0
        cmask = np.tile(cm.reshape(1, -1), (128, 1))
        rmask = np.tile(rm.reshape(1, -1), (128, 1))
        ph = np.zeros((L, 8, 2, 2, 4, 2, 64), f32)
        if sample:
            for ri, st in enumerate((A(state_ssm_re), A(state_ssm_im))):
                s = st[b]
                ph[:, :, ri] = s.reshape(L, 2, 8, 4, 2, 64).transpose(0, 2, 1, 3, 4, 5)
        parH = np.ascontiguousarray(ph.transpose(5, 6, 0, 1, 2, 3, 4).reshape(128, -1))
        return flags, cfix, parH, cmask, rmask

    n_used = _ncores
    roles = [(False, 0), (False, 1), (False, 2), (False, 3), (True, 0), (True, 1), (True, 0), (True, 1)][:n_used]
    if _roles is not None:
        roles = _roles
        n_used = len(roles)
    shared = dict(w_ada=A(w_ada), w_in=A(w_in), w_glu=A(w_glu), w_out=A(w_out), w_up=A(w_up), w_down=A(w_down),
                  pk16=pk16, bada=bada, pk8=pk8, cw=np.ascontiguousarray(cwp), fcw=np.ascontiguousarray(fcwp),
                  parR=parR, parCC=parCC)
    in_maps = []
    for (sample, b) in roles:
        if sample:
            xs = x_sample[b]
            cv = A(c)[b]
        else:
            xs = x_prompt[4 * b:4 * b + 4].reshape(1024, D)
            cv = A(c_ctx)
        flags, cfix, parH, cmask, rmask = role_arrays(sample, b)
        m = dict(shared)
        m.update(x0=_fm(xs), cvec=_vec(cv), flags=flags, cfix=cfix, parH=parH, cmask=cmask, rmask=rmask)
        in_maps.append(m)
    if 'nc' not in _NC_CACHE:
        _NC_CACHE['nc'] = build_nc()
    res = run_bass_kernel_spmd(_NC_CACHE['nc'], in_maps, core_ids=list(range(n_used)))
    outs = res.results
    if STOP[0] != 99:
        _NC_CACHE['dbg'] = [o.get("dbg") for o in outs]
        _NC_CACHE['raw'] = outs
    yp = np.zeros((16, 256, D), f32)
    ys = np.zeros((2, 1024, D), f32)
    nre = np.zeros((16, L, 2, 64, 64), f32)
    nim = np.zeros((16, L, 2, 64, 64), f32)
    for ci, (sample, b) in enumerate(roles[:min(6, n_used)]):
        yt = outs[ci]["y"]
        xt = yt.transpose(2, 1, 0).reshape(1024, D)
        if sample:
            ys[b] = xt
        else:
            yp[4 * b:4 * b + 4] = xt.reshape(4, 256, D)
            f = outs[ci]["fin"].reshape(2, 64, L, 8, 2, 4, 2, 4)
            f = f.transpose(7, 6, 2, 4, 3, 5, 0, 1).reshape(4, 2, L, 2, 64, 64)
            nre[4 * b:4 * b + 4] = f[:, 0]
            nim[4 * b:4 * b + 4] = f[:, 1]
    return yp, ys, nre, nim
```
